# Optimizing a Trainium2 kernel written in Bass

```python
import jax, jax.numpy as jnp
from jax import lax
import numpy as np

D_MODEL = 2048
BATCH = 4
SEQ = 4096
DEPTH = 1
DEC_BATCH = 16
DEC_SEQ = 64
PAST_LEN = 1024

CHUNK = 64
N_HEADS = 8
HEAD_DIM = 128
D_SB = N_HEADS * HEAD_DIM
POOL_WINDOWS = (2, 4, 8, 16)
N_POOL_GROUPS = len(POOL_WINDOWS)
D_POOL = 1024
POOL_GROUP = D_POOL // N_POOL_GROUPS
POOL_OUT = D_MODEL // N_POOL_GROUPS
POOL_PAST = max(POOL_WINDOWS) - 1
D_IN = 3 * D_SB + D_POOL + 2 * D_MODEL
D_FF = -(-8 * D_MODEL // (3 * 256)) * 256
DN_ALPHA = (2 * DEPTH) ** 0.25
DN_BETA = (8 * DEPTH) ** -0.25
Q_BLOCK = 128
LN_EPS = 1e-5

kernel_name = "stickbreak_pool_hybrid_stream_step"


def ln_plain(x):
    xf = x.astype(jnp.float32)
    mu = jnp.mean(xf, axis=-1, keepdims=True)
    xc = xf - mu
    var = jnp.mean(xc * xc, axis=-1, keepdims=True)
    return (xc * lax.rsqrt(var + LN_EPS)).astype(x.dtype)


def layer_norm(x, g, b):
    xf = x.astype(jnp.float32)
    mu = jnp.mean(xf, axis=-1, keepdims=True)
    xc = xf - mu
    var = jnp.mean(xc * xc, axis=-1, keepdims=True)
    y = xc * lax.rsqrt(var + LN_EPS) * g.astype(jnp.float32) + b.astype(jnp.float32)
    return y.astype(x.dtype)


def stick_breaking(q, k, v, q_pos, k_pos):
    z = jnp.einsum('bhqd,bhkd->bhqk', q, k).astype(jnp.float32) * (HEAD_DIM ** -0.5)
    mask = k_pos[None, :] < q_pos[:, None]
    log_beta = jax.nn.log_sigmoid(z)
    log_1mb = jnp.where(mask, jax.nn.log_sigmoid(-z), 0.0)
    after = lax.cumsum(log_1mb, axis=3, reverse=True) - log_1mb
    w = jnp.where(mask, jnp.exp(log_beta + after), 0.0)
    return jnp.einsum('bhqk,bhkd->bhqd', w.astype(v.dtype), v)


def multiscale_pool(p, past, pos0, w_pool, pool_scale):
    B, T, _ = p.shape
    full = jnp.concatenate([past, p], axis=1)
    ff = full.astype(jnp.float32)
    cs = jnp.concatenate([jnp.zeros((B, 1, D_POOL), jnp.float32), jnp.cumsum(ff, axis=1)], axis=1)
    pos = pos0 + jnp.arange(T)
    pf = p.astype(jnp.float32)
    outs = []
    for g, win in enumerate(POOL_WINDOWS):
        sl = slice(g * POOL_GROUP, (g + 1) * POOL_GROUP)
        hi = cs[:, POOL_PAST + 1:POOL_PAST + 1 + T, sl]
        lo = cs[:, POOL_PAST + 1 - win:POOL_PAST + 1 - win + T, sl]
        cnt = jnp.minimum(win, pos + 1).astype(jnp.float32)
        outs.append((hi - lo) / cnt[None, :, None] - pf[..., sl])
    d = jnp.stack(outs, axis=2).astype(p.dtype)
    y = jnp.einsum('btgc,gce->btge', d, w_pool).reshape(B, T, D_MODEL) * pool_scale
    return y, full[:, -POOL_PAST:, :]


def token_mixer(u, k_past, v_past, pool_past, pos0, w_in, w_sb_out, w_pool, pool_scale, w_out):
    B, T, _ = u.shape
    proj = u @ w_in
    q, k, v, p, ga, gb = jnp.split(
        proj, [D_SB, 2 * D_SB, 3 * D_SB, 3 * D_SB + D_POOL, 3 * D_SB + D_POOL + D_MODEL], axis=-1)
    heads = lambda t: t.reshape(B, T, N_HEADS, HEAD_DIM).transpose(0, 2, 1, 3)
    q, k, v = heads(q), heads(k), heads(v)
    if k_past is None:
        k_all, v_all = k, v
    else:
        k_all = jnp.concatenate([k_past, k], axis=2)
        v_all = jnp.concatenate([v_past, v], axis=2)
    q_pos = pos0 + jnp.arange(T)
    k_pos = jnp.arange(k_all.shape[2])
    if T % Q_BLOCK == 0:
        nb = T // Q_BLOCK
        qb = q.reshape(B, N_HEADS, nb, Q_BLOCK, HEAD_DIM).transpose(2, 0, 1, 3, 4)
        pb = q_pos.reshape(nb, Q_BLOCK)
        ob = lax.map(lambda a: stick_breaking(a[0], k_all, v_all, a[1], k_pos), (qb, pb))
        o = ob.transpose(1, 2, 0, 3, 4).reshape(B, N_HEADS, T, HEAD_DIM)
    else:
        o = stick_breaking(q, k_all, v_all, q_pos, k_pos)
    y_a = o.transpose(0, 2, 1, 3).reshape(B, T, D_SB) @ w_sb_out
    if pool_past is None:
        pool_past = jnp.zeros((B, POOL_PAST, D_POOL), p.dtype)
    y_b, pool_new = multiscale_pool(p, pool_past, pos0, w_pool, pool_scale)
    m = jax.nn.sigmoid(ga) * y_a + jax.nn.sigmoid(gb) * y_b
    return m @ w_out, k, v, pool_new


def block(x, c, k_past, v_past, pool_past, pos0, w_ada, b_ada, w_in, w_sb_out, w_pool,
          pool_scale, w_out, ln1_g, ln1_b, w_gate, w_up, w_down, ln2_g, ln2_b):
    B = x.shape[0]
    mod = (jax.nn.silu(c) @ w_ada + b_ada).reshape(B, 6, 1, D_MODEL)
    sh1, sc1, g1, sh2, sc2, g2 = (mod[:, i] for i in range(6))
    u = ln_plain(x) * (1 + sc1) + sh1
    mix, k_new, v_new, pool_new = token_mixer(u, k_past, v_past, pool_past, pos0, w_in,
                                              w_sb_out, w_pool, pool_scale, w_out)
    x = layer_norm(DN_ALPHA * x + g1 * mix, ln1_g, ln1_b)
    u = ln_plain(x) * (1 + sc2) + sh2
    f = (jax.nn.silu(u @ w_gate) * (u @ w_up)) @ w_down
    x = layer_norm(DN_ALPHA * x + g2 * f, ln2_g, ln2_b)
    return x, k_new, v_new, pool_new


def setup_inputs(seed: int = 0) -> dict:
    key = jax.random.key(seed)
    ks = jax.random.split(key, 24)
    nrm = lambda k, s: jax.random.normal(k, s, jnp.float32)
    w_in = nrm(ks[0], (DEPTH, D_MODEL, D_IN)) * D_MODEL ** -0.5
    w_in = w_in.at[..., 2 * D_SB:3 * D_SB].multiply(DN_BETA)
    return {
        "x_prompt": nrm(ks[1], (BATCH, SEQ, D_MODEL)),
        "x_sample": nrm(ks[2], (DEC_BATCH, DEC_SEQ, D_MODEL)),
        "c_prompt": nrm(ks[3], (BATCH, D_MODEL)),
        "c_sample": nrm(ks[4], (DEC_BATCH, D_MODEL)),
        "cache_k": nrm(ks[5], (DEPTH, DEC_BATCH, N_HEADS, PAST_LEN, HEAD_DIM)),
        "cache_v": nrm(ks[6], (DEPTH, DEC_BATCH, N_HEADS, PAST_LEN, HEAD_DIM)) * DN_BETA,
        "state_pool": nrm(ks[7], (DEPTH, DEC_BATCH, POOL_PAST, D_POOL)),
        "w_ada": nrm(ks[8], (DEPTH, D_MODEL, 6 * D_MODEL)) * D_MODEL ** -0.5,
        "b_ada": nrm(ks[9], (DEPTH, 6 * D_MODEL)) * 0.02,
        "w_in": w_in,
        "w_sb_out": nrm(ks[10], (DEPTH, D_SB, D_MODEL)) * D_SB ** -0.5,
        "w_pool": nrm(ks[11], (DEPTH, N_POOL_GROUPS, POOL_GROUP, POOL_OUT)) * POOL_GROUP ** -0.5,
        "pool_scale": 1.0 + 0.02 * nrm(ks[12], (DEPTH, D_MODEL)),
        "w_out": nrm(ks[13], (DEPTH, D_MODEL, D_MODEL)) * D_MODEL ** -0.5 * DN_BETA,
        "ln1_g": 1.0 + 0.02 * nrm(ks[14], (DEPTH, D_MODEL)),
        "ln1_b": 0.02 * nrm(ks[15], (DEPTH, D_MODEL)),
        "w_gate": nrm(ks[16], (DEPTH, D_MODEL, D_FF)) * D_MODEL ** -0.5,
        "w_up": nrm(ks[17], (DEPTH, D_MODEL, D_FF)) * D_MODEL ** -0.5,
        "w_down": nrm(ks[18], (DEPTH, D_FF, D_MODEL)) * D_FF ** -0.5 * DN_BETA,
        "ln2_g": 1.0 + 0.02 * nrm(ks[19], (DEPTH, D_MODEL)),
        "ln2_b": 0.02 * nrm(ks[20], (DEPTH, D_MODEL)),
    }


def reference(x_prompt, x_sample, c_prompt, c_sample, cache_k, cache_v, state_pool,
              w_ada, b_ada, w_in, w_sb_out, w_pool, pool_scale, w_out, ln1_g, ln1_b,
              w_gate, w_up, w_down, ln2_g, ln2_b):
    xp, xs = x_prompt, x_sample
    kp_l, vp_l, pp_l, ksm_l, vsm_l, psm_l = [], [], [], [], [], []
    for l in range(DEPTH):
        wl = (w_ada[l], b_ada[l], w_in[l], w_sb_out[l], w_pool[l], pool_scale[l], w_out[l],
              ln1_g[l], ln1_b[l], w_gate[l], w_up[l], w_down[l], ln2_g[l], ln2_b[l])
        xp, kp, vp, pp = block(xp, c_prompt, None, None, None, 0, *wl)
        xs, ksm, vsm, psm = block(xs, c_sample, cache_k[l], cache_v[l], state_pool[l], PAST_LEN, *wl)
        kp_l.append(kp); vp_l.append(vp); pp_l.append(pp)
        ksm_l.append(ksm); vsm_l.append(vsm); psm_l.append(psm)
    return (xp, xs, jnp.stack(kp_l), jnp.stack(vp_l), jnp.stack(pp_l),
            jnp.stack(ksm_l), jnp.stack(vsm_l), jnp.stack(psm_l))
```

```python
import numpy as np
from contextlib import ExitStack
import ml_dtypes
import concourse.bass as bass
import concourse.mybir as mybir
from concourse.bass_utils import run_bass_kernel_spmd

F32 = mybir.dt.float32
BF16 = mybir.dt.bfloat16
AF = mybir.ActivationFunctionType
ALU = mybir.AluOpType
ENGS = ("pe", "act", "dve", "pool", "sp")
D = 2048
DFF = 5632
ALPHA = 2.0 ** 0.25
SCALE = 128.0 ** -0.5
NOWN = 17
ST = 2
DEBUG = False
import os
SKIP = os.environ.get("KSKIP", "")
DBG_OUT = {}


import types


def _snap(fn):
    if fn.__closure__ is None:
        return fn
    cells = []
    for c_ in fn.__closure__:
        try:
            cells.append(types.CellType(c_.cell_contents))
        except ValueError:
            cells.append(c_)
    return types.FunctionType(fn.__code__, fn.__globals__, fn.__name__, fn.__defaults__, tuple(cells))


class Op:
    __slots__ = ("eng", "fn", "kind", "deps", "signal", "sigval", "sem", "idx", "prewait")

    def __init__(self, eng, fn, kind):
        self.eng, self.fn, self.kind = eng, _snap(fn), kind
        self.deps = []
        self.signal = False
        self.sigval = 0
        self.sem = None
        self.prewait = None


class Prog:
    NDMASEM = 12

    def __init__(self, nc):
        self.nc = nc
        self.ops = {e: [] for e in ENGS}
        self.last_w = {}
        self.readers = {}
        self.ndma = {e: 0 for e in ENGS}
        self.all_dma = []
        self._fence_dma = 0

    def _add(self, op, reads, writes):
        deps = []
        for r in reads:
            w = self.last_w.get(r)
            if w is not None:
                deps.append((w, True))
        for w_ in writes:
            w = self.last_w.get(w_)
            if w is not None:
                deps.append((w, False))
            deps.extend((x, False) for x in self.readers.get(w_, ()))
        seen = set()
        for d, raw in deps:
            if d is op or id(d) in seen:
                continue
            if d.eng == op.eng and d.kind == "c" and op.kind == "c" and not raw:
                continue
            seen.add(id(d))
            op.deps.append(d)
        for r in reads:
            self.readers.setdefault(r, []).append(op)
        for w_ in writes:
            self.last_w[w_] = op
            self.readers[w_] = []
        self.ops[op.eng].append(op)
        return op

    def c(self, eng, fn, reads=(), writes=()):
        return self._add(Op(eng, fn, "c"), reads, writes)

    def d(self, eng, fn, reads=(), writes=()):
        op = Op(eng, fn, "d")
        op.idx = self.ndma[eng]
        self.ndma[eng] += 1
        self.all_dma.append(op)
        return self._add(op, reads, writes)

    def fence(self):
        lasts = []
        for e in ENGS:
            cs = [o for o in self.ops[e] if o.kind == "c"]
            if cs:
                lasts.append(cs[-1])
        dmas = list(self.all_dma[self._fence_dma:])
        self._fence_dma = len(self.all_dma)
        for e in ENGS:
            op = Op(e, (lambda eng: eng.nop()), "c")
            op.deps = [d for d in lasts + dmas]
            self.ops[e].append(op)

    def emit(self):
        nc = self.nc
        for e in ENGS:
            for op in self.ops[e]:
                for d in op.deps:
                    if d.kind == "c":
                        if d.eng == op.eng and op.kind == "c" and e == "pe":
                            continue
                        d.signal = True
        csem = {e: nc.alloc_semaphore(name=f"c_{e}") for e in ENGS}
        dsem = {e: [nc.alloc_semaphore(name=f"d_{e}_{i}") for i in range(self.NDMASEM)]
                for e in ENGS if self.ndma[e] > 0}
        P = self.NDMASEM
        for e in ENGS:
            cnt = 0
            for op in self.ops[e]:
                if op.kind == "c":
                    if op.signal:
                        cnt += 1
                        op.sigval = cnt
                        op.sem = csem[e]
                else:
                    op.sem = dsem[e][op.idx % P]
                    op.sigval = 16 * (op.idx // P + 1)
                    if op.idx >= P:
                        op.prewait = (op.sem, 16 * (op.idx // P))
        final_dma = {}
        for op in self.all_dma:
            final_dma[id(op.sem)] = (op.sem, max(op.sigval, final_dma.get(id(op.sem), (None, 0))[1]))
        with nc.Block() as block:
            def run(e):
                def body(eng):
                    waited = {}

                    def wait(sem, val):
                        k = id(sem)
                        if waited.get(k, 0) >= val:
                            return
                        waited[k] = val
                        eng.wait_ge(sem, val)

                    for op in self.ops[e]:
                        need = {}
                        if op.prewait is not None:
                            need[id(op.prewait[0])] = op.prewait
                        for d in op.deps:
                            if d.kind == "c" and not d.signal:
                                continue
                            k = id(d.sem)
                            if k not in need or need[k][1] < d.sigval:
                                need[k] = (d.sem, d.sigval)
                        for sem, val in need.values():
                            wait(sem, val)
                        ins = op.fn(eng)
                        if op.kind == "d":
                            ins.then_inc(op.sem, 16)
                        elif op.signal:
                            ins.then_inc(op.sem, 1)
                    if e == "sp":
                        for sem, val in final_dma.values():
                            wait(sem, val)
                return body
            block.tensor(run("pe"))
            block.scalar(run("act"))
            block.vector(run("dve"))
            block.gpsimd(run("pool"))
            block.sync(run("sp"))


def build_nc():
    nc = bass.Bass("TRN2", target_bir_lowering=False)

    def din(name, shape, dt=F32):
        return nc.dram_tensor(name, list(shape), dt, kind="ExternalInput").ap()

    def dout(name, shape):
        return nc.dram_tensor(name, list(shape), F32, kind="ExternalOutput").ap()

    xseq = din("xseq", [4096, D]); xown = din("xown", [NOWN * 128, D]); xprev = din("xprev", [NOWN * 16, D])
    cT = din("cT", [128, 16, 3]); ck = din("ck", [2, 8, 1024, 128]); cv = din("cv", [2, 8, 1024, 128])
    spool = din("spool", [2, 15, 1024])
    w_ada = din("w_ada", [D, 6 * D]); badaT = din("badaT", [128, 96]); bada = din("bada", [1, 6 * D])
    w_in = din("w_in", [D, 8192]); w_sb = din("w_sb", [1024, D]); w_pool = din("w_pool", [1024, 512])
    psT = din("psT", [128, 16]); w_out = din("w_out", [D, D])
    ln1g = din("ln1g", [1, D]); ln1b = din("ln1b", [1, D]); ln2g = din("ln2g", [1, D]); ln2b = din("ln2b", [1, D])
    w_gate = din("w_gate", [D, DFF]); w_up = din("w_up", [D, DFF]); w_down = din("w_down", [DFF, D])
    ident_b = din("ident_b", [128, 128], BF16); ident_f = din("ident_f", [128, 128])
    tri_b = din("tri_b", [128, 128], BF16); ones_b = din("ones_b", [128, 128], BF16)
    maskp = din("maskp", [128, 4, 256], BF16); masks = din("masks", [64, 64], BF16)
    bands = din("bands", [NOWN, 160, 4, 128], BF16)
    yown = dout("yown", [NOWN * 128, D]); kseq = dout("kseq", [4096, 1024]); vseq = dout("vseq", [4096, 1024])
    ksmp = dout("ksmp", [128, 1024]); vsmp = dout("vsmp", [128, 1024]); pout = dout("pout", [2, 128, 1024])
    gscr = nc.dram_tensor("gscr", [3, 2, D], F32).ap()
    NWT = 52
    wscr = nc.dram_tensor("wscr", [NWT, 128, 16 * 512], BF16).ap()

    P = Prog(nc)
    dbg_names = []

    def dump(name, ap_, shape, dt, reads):
        if not DEBUG:
            return
        t_ = nc.dram_tensor("dbg_" + name, list(shape), dt, kind="ExternalOutput").ap()
        dbg_names.append("dbg_" + name)
        P.d("sp", lambda e: e.dma_start(out=t_, in_=ap_), reads=reads)

    with ExitStack() as top:
        uid = [0]

        def SB(es, name, shape, dt):
            uid[0] += 1
            return es.enter_context(nc.sbuf_tensor(f"{name}_{uid[0]}", list(shape), dt))

        def PS(es, name, shape, dt):
            uid[0] += 1
            return es.enter_context(nc.psum_tensor(f"{name}_{uid[0]}", list(shape), dt))

        idb = SB(top, "idb", [128, 128], BF16); idf = SB(top, "idf", [128, 128], F32)
        tri = SB(top, "tri", [128, 128], BF16); ones = SB(top, "ones", [128, 128], BF16)
        modT = SB(top, "modT", [128, 6, 16, 3], F32)
        pst = SB(top, "pst", [128, 16], F32)
        mhalf = SB(top, "mhalf", [128, 1], F32)
        QoT = SB(top, "QoT", [128, 8, NOWN * 128], BF16)
        stats = SB(top, "stats", [128, 2, 4, 6], F32); mv = SB(top, "mv", [128, 2, 2], F32)
        veps = SB(top, "veps", [128, 2, 1], F32); rstd = SB(top, "rstd", [128, 2, 1], F32)
        nmr = SB(top, "nmr", [128, 2, 1], F32)
        xn = SB(top, "xn", [128, 2, D], BF16)
        for t_, src, key_ in ((idb, ident_b, "idb"), (idf, ident_f, "idf"), (tri, tri_b, "tri"), (ones, ones_b, "ones"),
                              (pst, psT, "pst")):
            P.d("sp", lambda e, t_=t_, src=src: e.dma_start(out=t_[:], in_=src), writes=[key_])
        P.c("pool", lambda e: e.memset(mhalf[:], -0.5), writes=["mhalf"])
        cnt = {"st": 0, "w": 0, "bc": 0, "xn": 0}

        def ln_stats(xa, xkey, n):
            s = cnt["st"] % 2
            cnt["st"] += 1
            k_ = f"st{s}"
            for q in range(4):
                P.c("dve", lambda e, q=q: e.bn_stats(out=stats[0:n, s, q, :], in_=xa[:, q * 512:(q + 1) * 512]),
                    reads=[xkey], writes=[k_ + "a"])
            P.c("dve", lambda e: e.bn_aggr(out=mv[0:n, s, :], in_=stats[0:n, s, :, :]), reads=[k_ + "a"], writes=[k_ + "b"])
            P.c("dve", lambda e: e.tensor_scalar_add(veps[0:n, s, :], mv[0:n, s, 1:2], 1e-5), reads=[k_ + "b"], writes=[k_ + "c"])
            P.c("pool", lambda e: e.tensor_tensor(out=rstd[0:n, s, :], in0=veps[0:n, s, :], in1=mhalf[0:n, :], op=ALU.pow),
                reads=[k_ + "c", "mhalf"], writes=[k_ + "d"])
            P.c("dve", lambda e: e.scalar_tensor_tensor(out=nmr[0:n, s, :], in0=mv[0:n, s, 0:1], scalar=-1.0,
                                                        in1=rstd[0:n, s, :], op0=ALU.mult, op1=ALU.mult),
                reads=[k_ + "b", k_ + "d"], writes=[k_ + "e"])
            return s, [k_ + "d", k_ + "e"]

        def lnA(xa, xkey, n):
            s, keys = ln_stats(xa, xkey, n)
            xs = cnt["xn"] % 2
            cnt["xn"] += 1
            P.c("act", lambda e: e.activation(out=xn[0:n, xs, :], in_=xa, func=AF.Identity,
                                              bias=nmr[0:n, s, :], scale=rstd[0:n, s, :]),
                reads=[xkey] + keys, writes=[f"xn{xs}"])
            return xs

        def lnB(tb, xs, n, vsc, vsh, segs, dst, dkey, doff):
            for k in range(16):
                b_ = k // 8
                P.c("pe", lambda e, k=k, b_=b_: e.transpose(out=tb[b_][:, k % 8, 0:n], in_=xn[0:n, xs, k * 128:(k + 1) * 128],
                                                            identity=idb[0:n, 0:n]),
                    reads=[f"xn{xs}", "idb"], writes=[f"TB{b_}"])
            for k in range(16):
                b_ = k // 8
                for (lo, hi, sq) in segs:
                    if b_ == 0:
                        P.c("act", lambda e, k=k, lo=lo, hi=hi, sq=sq: e.activation(
                            out=dst[:, k, doff + lo:doff + hi], in_=tb[0][:, k % 8, lo:hi], func=AF.Identity,
                            bias=modT[:, vsh, k, sq:sq + 1], scale=modT[:, vsc, k, sq:sq + 1]),
                            reads=["modT"], writes=["TB0", dkey])
                    else:
                        P.c("dve", lambda e, k=k, lo=lo, hi=hi, sq=sq: e.tensor_scalar(
                            out=dst[:, k, doff + lo:doff + hi], in0=tb[1][:, k % 8, lo:hi],
                            scalar1=modT[:, vsc, k, sq:sq + 1], scalar2=modT[:, vsh, k, sq:sq + 1],
                            op0=ALU.mult, op1=ALU.add),
                            reads=["modT"], writes=["TB1", dkey])

        def ln0(tb, xa, xkey, n, vsc, vsh, segs, dst, dkey, doff):
            xs = lnA(xa, xkey, n)
            lnB(tb, xs, n, vsc, vsh, segs, dst, dkey, doff)

        def wload(wt, src, nk, ncols, key):
            P.d("pool", lambda e: e.dma_start(out=wt[:, 0:nk, 0:ncols], in_=src.rearrange("(dc p) n -> p dc n", p=128)),
                writes=[key])

        def seg_for(i):
            return [(0, 128, 0)] if i < 16 else [(0, 64, 1), (64, 128, 2)]

        wtiles = {}
        for c2 in range(2):
            wtiles[c2] = (w_in[:, 3072 + c2 * 512:3072 + (c2 + 1) * 512], 16)
        for c4 in range(4):
            wtiles[2 + c4] = (w_in[:, 4096 + c4 * 512:4096 + (c4 + 1) * 512], 16)
            wtiles[6 + c4] = (w_in[:, 6144 + c4 * 512:6144 + (c4 + 1) * 512], 16)
            wtiles[10 + c4] = (w_sb[:, c4 * 512:(c4 + 1) * 512], 8)
            wtiles[14 + c4] = (w_out[:, c4 * 512:(c4 + 1) * 512], 16)
            for pc_ in range(3):
                wtiles[40 + c4 * 3 + pc_] = (w_down[pc_ * 2048:min((pc_ + 1) * 2048, DFF), c4 * 512:(c4 + 1) * 512], 16 if pc_ < 2 else 12)
        for c11 in range(11):
            wtiles[18 + c11] = (w_gate[:, c11 * 512:(c11 + 1) * 512], 16)
            wtiles[29 + c11] = (w_up[:, c11 * 512:(c11 + 1) * 512], 16)
        conv_todo = sorted(wtiles.keys())
        wcache = {}
        EARLY = os.environ.get("KEARLY", "1") == "1"

        def conv_one():
            if not EARLY or not conv_todo:
                return
            tid = conv_todo.pop(0)
            src, nk = wtiles[tid]
            dst = wscr[tid].rearrange("p (k n) -> p k n", n=512)[:, 0:nk, :]
            P.d("pool", lambda e: e.dma_start(out=dst, in_=src.rearrange("(dc p) n -> p dc n", p=128)), writes=[f"wscr{tid}"])
            wcache[tid] = True

        with ExitStack() as es:
            ct32 = SB(es, "ct32", [128, 16, 3], F32); cex = SB(es, "cex", [128, 16, 3], F32)
            scT = SB(es, "scT", [128, 16, 3], BF16)
            bT = SB(es, "bT", [128, 96], F32); brow = SB(es, "brow", [3, 6 * D], F32)
            wa = [SB(es, f"wa{i}", [128, 16, 512], BF16) for i in range(2)]
            grow = SB(es, "grow", [3, 2, 512], F32)
            pg = PS(es, "pg", [128, 512], F32); pm = PS(es, "pm", [128, 6, 16, 3], F32)
            P.d("sp", lambda e: e.dma_start(out=ct32[:], in_=cT), writes=["ct32"])
            P.d("sp", lambda e: e.dma_start(out=bT[:], in_=badaT), writes=["bT"])
            P.d("sp", lambda e: e.dma_start(out=brow[:], in_=bada.partition_broadcast(3)), writes=["brow"])
            P.c("act", lambda e: e.activation(out=cex[:], in_=ct32[:], func=AF.Exp, scale=-1.0), reads=["ct32"], writes=["cex"])
            P.c("dve", lambda e: e.tensor_scalar_add(cex[:], cex[:], 1.0), reads=["cex"], writes=["cex"])
            P.c("dve", lambda e: e.reciprocal(cex[:], cex[:]), reads=["cex"], writes=["cex"])
            P.c("dve", lambda e: e.tensor_tensor(out=scT[:], in0=ct32[:], in1=cex[:], op=ALU.mult), reads=["cex", "ct32"], writes=["scT"])
            for ct in range(24):
                v, q4 = ct // 4, ct % 4
                w_ = wa[ct % 2]
                wk = f"wa{ct % 2}"
                wload(w_, w_ada[:, ct * 512:(ct + 1) * 512], 16, 512, wk)
                if v in (2, 5):
                    for dc in range(16):
                        P.c("pe", lambda e, dc=dc, w_=w_: e.matmul(pg[0:3, :], lhsT=scT[:, dc, :], rhs=w_[:, dc, :],
                                                                   start=(dc == 0), stop=(dc == 15)),
                            reads=[wk, "scT"], writes=["PG"])
                    vi = 0 if v == 2 else 1
                    P.c("dve", lambda e, vi=vi, ct=ct: e.tensor_tensor(out=grow[:, vi, :], in0=pg[0:3, :],
                                                                       in1=brow[:, ct * 512:(ct + 1) * 512], op=ALU.add),
                        reads=["brow"], writes=["PG", "grow"])
                    P.d("sp", lambda e, vi=vi, q4=q4: e.dma_start(out=gscr[:, vi, q4 * 512:(q4 + 1) * 512], in_=grow[:, vi, :]),
                        reads=["grow"], writes=["gscr"])
                else:
                    for fs in range(4):
                        for dc in range(16):
                            P.c("pe", lambda e, dc=dc, fs=fs, w_=w_, v=v, q4=q4: e.matmul(
                                pm[:, v, q4 * 4 + fs, :], lhsT=w_[:, dc, fs * 128:(fs + 1) * 128], rhs=scT[:, dc, :],
                                start=(dc == 0), stop=(dc == 15)), reads=[wk, "scT"], writes=["PM"])
            pmv = pm[:].rearrange("p a b c -> p (a b) c")
            mdv = modT[:].rearrange("p a b c -> p (a b) c")
            for v in (0, 1, 3, 4):
                for sq in range(3):
                    P.c("dve", lambda e, sq=sq, v=v: e.tensor_tensor(out=modT[:, v, :, sq], in0=pm[:, v, :, sq], in1=bT[:, v * 16:(v + 1) * 16], op=ALU.add),
                        reads=["bT"], writes=["PM", "modT"])
            for v in (1, 4):
                P.c("dve", lambda e, v=v: e.tensor_scalar_add(modT[:, v, :, :], modT[:, v, :, :], 1.0), reads=["modT"], writes=["modT"])

        for hg in range(2):
            P.fence()
            with ExitStack() as eg:
                KT = SB(eg, "KT", [128, 4, 4096], BF16); Vt = SB(eg, "Vt", [128, 32, 512], BF16)
                ksn = SB(eg, "ksn", [128, 2, 4, 64], BF16); vsn = SB(eg, "vsn", [64, 2, 512], BF16)
                mkp = SB(eg, "mkp", [128, 4, 256], BF16); mks = SB(eg, "mks", [64, 64], BF16)
                P.d("sp", lambda e: e.dma_start(out=mkp[:], in_=maskp), writes=["mkp"])
                P.d("sp", lambda e: e.dma_start(out=mks[:], in_=masks), writes=["mks"])
                with ExitStack() as es:
                    Wq = SB(es, "Wq", [128, 16, 512], BF16); Wk = SB(es, "Wk", [128, 16, 512], BF16)
                    Wv = SB(es, "Wv", [128, 16, 512], BF16)
                    xt = SB(es, "xt", [128, 2, D], F32); uT = SB(es, "uT", [128, 2, 16, 128], BF16)
                    k32 = SB(es, "k32", [128, 2, 512], F32); v32 = SB(es, "v32", [128, 2, 512], F32)
                    tb = [PS(es, f"tb{i}", [128, 8, 128], BF16) for i in range(2)]
                    pk = PS(es, "pk", [128, 512], F32); pv = PS(es, "pv", [128, 512], F32)
                    pq = PS(es, "pq", [128, 4, 128], F32); pt = PS(es, "pt", [128, 4, 128], F32)
                    pdum1 = PS(es, "pdum1", [128, 512], F32)
                    NDUM1 = int(os.environ.get("KDUM1", "0"))
                    wload(Wq, w_in[:, hg * 512:(hg + 1) * 512], 16, 512, "Wq")
                    wload(Wk, w_in[:, 1024 + hg * 512:1024 + (hg + 1) * 512], 16, 512, "Wk")
                    wload(Wv, w_in[:, 2048 + hg * 512:2048 + (hg + 1) * 512], 16, 512, "Wv")

                    def kv_block(u_, ukey, n, m0, s2, kdst, kkey, vdst, vkey, kout, vout):
                        for dc in range(16):
                            P.c("pe", lambda e, dc=dc: e.matmul(pk[0:n, :], lhsT=u_[:, dc, m0:m0 + n], rhs=Wk[:, dc, :],
                                                                start=(dc == 0), stop=(dc == 15)), reads=[ukey, "Wk"], writes=["PK"])
                        P.c("act", lambda e: e.copy(out=k32[0:n, s2, :], in_=pk[0:n, :]), writes=["PK", f"k32{s2}"])
                        for dc in range(16):
                            P.c("pe", lambda e, dc=dc: e.matmul(pv[0:n, :], lhsT=u_[:, dc, m0:m0 + n], rhs=Wv[:, dc, :],
                                                                start=(dc == 0), stop=(dc == 15)), reads=[ukey, "Wv"], writes=["PV"])
                        P.c("dve", lambda e: e.tensor_copy(out=v32[0:n, s2, :], in_=pv[0:n, :]), writes=["PV", f"v32{s2}"])
                        P.c("pool", lambda e: e.tensor_copy(out=vdst, in_=v32[0:n, s2, :]), reads=[f"v32{s2}"], writes=[vkey])
                        P.d("sp", lambda e: e.dma_start(out=kout, in_=k32[0:n, s2, :]), reads=[f"k32{s2}"])
                        P.d("sp", lambda e: e.dma_start(out=vout, in_=v32[0:n, s2, :]), reads=[f"v32{s2}"])
                        for h in range(4):
                            P.c("pe", lambda e, h=h: e.transpose(out=pt[:, h, 0:n], in_=k32[0:n, s2, h * 128:(h + 1) * 128],
                                                                 identity=idf[0:n, 0:n]), reads=[f"k32{s2}", "idf"], writes=["PT"])
                        P.c("dve", lambda e: e.tensor_copy(out=kdst, in_=pt[:, :, 0:n]), writes=["PT", kkey])

                    items = [("g", gb) for gb in range(32)] + [("o", i) for i in range(NOWN)]
                    NI = len(items)
                    xsl = {}

                    def stageA(it):
                        kind, ix = items[it]
                        s2 = it % 2
                        src = xseq if kind == "g" else xown
                        P.d("sp", lambda e: e.dma_start(out=xt[:, s2, :], in_=src[ix * 128:(ix + 1) * 128, :]), writes=[f"xt{s2}"])
                        xsl[it] = lnA(xt[:, s2, :], f"xt{s2}", 128)

                    def stageB(it):
                        kind, ix = items[it]
                        s2 = it % 2
                        segs = [(0, 128, 0)] if kind == "g" else seg_for(ix)
                        lnB(tb, xsl[it], 128, 1, 0, segs, uT[:, s2], f"uT{s2}", 0)

                    def stageC(it):
                        kind, ix = items[it]
                        s2 = it % 2
                        if kind == "g":
                            gb = ix
                            kv_block(uT[:, s2], f"uT{s2}", 128, 0, s2, KT[:, :, gb * 128:(gb + 1) * 128], "KT",
                                     Vt[:, gb, :], "Vt",
                                     kseq[gb * 128:(gb + 1) * 128, hg * 512:(hg + 1) * 512],
                                     vseq[gb * 128:(gb + 1) * 128, hg * 512:(hg + 1) * 512])
                            return
                        i = ix
                        for h in range(4):
                            for dc in range(16):
                                P.c("pe", lambda e, h=h, dc=dc: e.matmul(
                                    pq[:, h, :], lhsT=Wq[:, dc, h * 128:(h + 1) * 128], rhs=uT[:, s2, dc, :],
                                    start=(dc == 0), stop=(dc == 15)), reads=[f"uT{s2}", "Wq"], writes=["PQ"])
                        qk = [f"Q{hg * 4 + h}_{(i // 2) if i < 16 else 8 + ss}" for h in range(4) for ss in ((0,) if i < 16 else (0, 1))]
                        P.c("act", lambda e: e.copy(out=QoT[:, hg * 4:hg * 4 + 4, i * 128:(i + 1) * 128], in_=pq[:]),
                            writes=["PQ"] + qk)
                        if i == 16:
                            for ss in range(2):
                                kv_block(uT[:, s2], f"uT{s2}", 64, ss * 64, ss, ksn[:, ss, :, :], "ksn", vsn[:, ss, :], "vsn",
                                         ksmp[ss * 64:(ss + 1) * 64, hg * 512:(hg + 1) * 512],
                                         vsmp[ss * 64:(ss + 1) * 64, hg * 512:(hg + 1) * 512])

                    for st_ in range(NI + 2):
                        for _ in range(NDUM1):
                            P.c("pe", lambda e: e.matmul(pdum1[:, :], lhsT=ones[:, :], rhs=Wq[:, 0, :], start=True, stop=True),
                                reads=["ones", "Wq"], writes=["PDUM1"])
                        if st_ < NI:
                            stageA(st_)
                        if 1 <= st_ <= NI:
                            stageB(st_ - 1)
                        if 2 <= st_:
                            stageC(st_ - 2)

                P.fence()
                with ExitStack() as es:
                    Et = SB(es, "Et", [128, 3, 256], F32); SPb = SB(es, "SPb", [128, 3, 256], BF16)
                    Xt = SB(es, "Xt", [128, 3, 256], BF16); Gt = SB(es, "Gt", [128, 2, 256], F32)
                    wT = SB(es, "wT", [128, 3, 256], BF16)
                    cst = SB(es, "cst", [128, 8, 128], F32)
                    KsT = SB(es, "KsT", [128, 4, 1024], BF16); Vs = SB(es, "Vs", [128, 8, 512], BF16)
                    Zb = [PS(es, f"Zb{i}", [128, 512], F32) for i in range(2)]
                    Ab = [PS(es, f"Ab{i}", [128, 512], F32) for i in range(2)]
                    Ob = [PS(es, f"Ob{i}", [128, 512], F32) for i in range(2)]
                    pc = PS(es, "pc", [128, 4, 128], F32)
                    pdum = PS(es, "pdum", [128, 512], F32)
                    NDUM = int(os.environ.get("KDUM", "2"))

                    def run_jobs(steps):
                        n_ = len(steps)
                        for it in range(-1, n_ + 3):
                            if it % 22 == 0:
                                conv_one()
                            j = it + 1
                            if 0 <= j < n_:
                                s = steps[j]
                                P.c("pe", lambda e, s=s, j=j: e.matmul(Zb[j % 2][0:s["nk"], 0:s["nq"]], lhsT=s["kT"], rhs=s["q"],
                                                                      start=True, stop=True),
                                    reads=[s["kkey"], s["qkey"]], writes=[f"Z{j % 2}"])
                            for _ in range(NDUM):
                                P.c("pe", lambda e: e.matmul(pdum[:, 0:256], lhsT=ones[:, :], rhs=tri[:, :].to_broadcast([128, 256]) if False else mkp[:, 0, :],
                                                             start=True, stop=True), reads=["ones", "mkp"], writes=["PDUM"])
                            j = it - 1
                            if 0 <= j < n_:
                                s = steps[j]
                                nk, nq = s["nk"], s["nq"]
                                P.c("pe", lambda e, s=s, j=j, nk=nk, nq=nq: e.matmul(
                                    Ab[j % 2][0:nk, 0:nq], lhsT=tri[0:nk, 0:nk], rhs=SPb[0:nk, j % 3, 0:nq],
                                    start=True, stop=(s["t"] == 0)), reads=[f"SP{j % 3}", "tri"], writes=[f"A{j % 2}"])
                                if s["t"] > 0:
                                    xs_ = s["xsrc"]
                                    P.c("pe", lambda e, j=j, nk=nk, nq=nq, xs_=xs_: e.matmul(
                                        Ab[j % 2][0:nk, 0:nq], lhsT=ones[:, 0:nk], rhs=Xt[:, xs_, 0:nq],
                                        start=False, stop=True), reads=[f"X{xs_}", "ones"], writes=[f"A{j % 2}"])
                            j = it - 2
                            if 0 <= j < n_:
                                s = steps[j]
                                nk, nq, ob = s["nk"], s["nq"], s["ob"]
                                P.c("pe", lambda e, s=s, j=j, nk=nk, nq=nq, ob=ob: e.matmul(
                                    Ob[ob][:, 0:nq], lhsT=s["v"], rhs=wT[0:nk, j % 3, 0:nq],
                                    start=(s["t"] == 0), stop=s["last"]), reads=[f"w{j % 3}", s["vkey"]], writes=[f"O{ob}"])
                                if s["last"]:
                                    P.c("dve", lambda e, s=s, nq=nq, ob=ob: e.tensor_copy(out=s["out"], in_=Ob[ob][:, 0:nq]),
                                        writes=[f"O{ob}", s["qkey"]])
                            j = it
                            if 0 <= j < n_:
                                s = steps[j]
                                nk, nq = s["nk"], s["nq"]
                                P.c("act", lambda e, j=j, nk=nk, nq=nq: e.activation(out=Et[0:nk, j % 3, 0:nq], in_=Zb[j % 2][0:nk, 0:nq],
                                                                                   func=AF.Exp, scale=SCALE),
                                    writes=[f"Z{j % 2}", f"E{j % 3}"])
                                P.c("act", lambda e, j=j, nk=nk, nq=nq: e.activation(out=SPb[0:nk, j % 3, 0:nq], in_=Et[0:nk, j % 3, 0:nq],
                                                                                   func=AF.Ln, bias=1.0),
                                    reads=[f"E{j % 3}"], writes=[f"SP{j % 3}"])
                                if s["mask"] is not None:
                                    P.c("pool", lambda e, s=s, j=j, nk=nk, nq=nq: e.tensor_tensor(
                                        out=SPb[0:nk, j % 3, 0:nq], in0=SPb[0:nk, j % 3, 0:nq], in1=s["mask"], op=ALU.mult),
                                        reads=["mkp", "mks"], writes=[f"SP{j % 3}"])
                                if not s["last"]:
                                    xd = s["xdst"]
                                    if s["t"] == 0:
                                        if nk < 128:
                                            P.c("pool", lambda e, xd=xd: e.memset(Xt[:, xd, :], 0.0), writes=[f"X{xd}"])
                                        P.c("pool", lambda e, j=j, nk=nk, nq=nq, xd=xd: e.tensor_copy(out=Xt[0:nk, xd, 0:nq], in_=SPb[0:nk, j % 3, 0:nq]),
                                            reads=[f"SP{j % 3}"] + ([f"X{xd}"] if nk < 128 else []), writes=[f"X{xd}"])
                                    else:
                                        xs_ = s["xsrc"]
                                        P.c("pool", lambda e, j=j, nq=nq, xd=xd, xs_=xs_: e.tensor_tensor(
                                            out=Xt[:, xd, 0:nq], in0=Xt[:, xs_, 0:nq], in1=SPb[:, j % 3, 0:nq], op=ALU.add),
                                            reads=[f"SP{j % 3}", f"X{xs_}"], writes=[f"X{xd}"])
                            j = it - 1
                            if 0 <= j < n_:
                                s = steps[j]
                                nk, nq = s["nk"], s["nq"]
                                P.c("act", lambda e, j=j, nk=nk, nq=nq: e.activation(out=Gt[0:nk, j % 2, 0:nq], in_=Ab[j % 2][0:nk, 0:nq],
                                                                                   func=AF.Exp, scale=-1.0),
                                    writes=[f"A{j % 2}", f"G{j % 2}"])
                                P.c("dve", lambda e, j=j, nk=nk, nq=nq: e.tensor_tensor(out=wT[0:nk, j % 3, 0:nq], in0=Et[0:nk, j % 3, 0:nq],
                                                                                      in1=Gt[0:nk, j % 2, 0:nq], op=ALU.mult),
                                    reads=[f"E{j % 3}", f"G{j % 2}"], writes=[f"w{j % 3}"])
                                if s["mask"] is not None:
                                    P.c("dve", lambda e, s=s, j=j, nk=nk, nq=nq: e.tensor_tensor(
                                        out=wT[0:nk, j % 3, 0:nq], in0=wT[0:nk, j % 3, 0:nq], in1=s["mask"], op=ALU.mult),
                                        reads=["mkp", "mks", f"w{j % 3}"], writes=[f"w{j % 3}"])

                    def finish(steps, njob):
                        xc = 0
                        t = 0
                        for s in steps:
                            s["t"] = t
                            s["ob"] = njob % 2
                            s["xsrc"] = (xc - 1) % 3
                            s["xdst"] = xc % 3
                            xc += 1
                            t += 1
                        return steps

                    steps = []
                    njob = 0
                    for jp in range(8):
                        for h in range(4):
                            hh = hg * 4 + h
                            job = []
                            nkb = 4 * jp + 4
                            for kb in range(nkb - 1, -1, -1):
                                r = kb - 4 * jp
                                job.append(dict(kT=KT[:, h, kb * 128:(kb + 1) * 128], v=Vt[:, kb, h * 128:(h + 1) * 128],
                                                q=QoT[:, hh, jp * 256:(jp + 1) * 256], mask=(mkp[:, r, :] if r >= 0 else None),
                                                nk=128, nq=256, last=(kb == 0), out=QoT[:, hh, jp * 256:(jp + 1) * 256],
                                                kkey="KT", vkey="Vt", qkey=f"Q{hh}_{jp}"))
                            steps += finish(job, njob)
                            njob += 1
                    run_jobs(steps)
                    for ss in range(2):
                        for h in range(4):
                            hh = hg * 4 + h
                            P.d("sp", lambda e, ss=ss, hh=hh: e.dma_start(out=cst[:], in_=ck[ss, hh].rearrange("(b p) d -> p b d", p=128)),
                                writes=["cst"])
                            for half in range(2):
                                for b4 in range(4):
                                    P.c("pe", lambda e, half=half, b4=b4: e.transpose(out=pc[:, b4, :], in_=cst[:, half * 4 + b4, :], identity=idf[:]),
                                        reads=["cst", "idf"], writes=["PC"])
                                P.c("dve", lambda e, h=h, half=half: e.tensor_copy(
                                    out=KsT[:, h, half * 512:(half + 1) * 512], in_=pc[:].rearrange("p a b -> p (a b)")),
                                    writes=["PC", "KsT"])
                            P.d("sp", lambda e, ss=ss, hh=hh: e.dma_start(out=cst[:], in_=cv[ss, hh].rearrange("(b p) d -> p b d", p=128)),
                                writes=["cst"])
                            P.c("pool", lambda e, h=h: e.tensor_copy(out=Vs[:, :, h * 128:(h + 1) * 128], in_=cst[:]), reads=["cst"], writes=["Vs"])
                        steps = []
                        for h in range(4):
                            hh = hg * 4 + h
                            q_ = QoT[:, hh, 2048 + ss * 64:2048 + (ss + 1) * 64]
                            job = [dict(kT=ksn[:, ss, h, :], v=vsn[:, ss, h * 128:(h + 1) * 128], q=q_, mask=mks[:, :], nk=64, nq=64,
                                        last=False, out=q_, kkey="ksn", vkey="vsn", qkey=f"Q{hh}_{8 + ss}")]
                            for kb in range(7, -1, -1):
                                job.append(dict(kT=KsT[:, h, kb * 128:(kb + 1) * 128], v=Vs[:, kb, h * 128:(h + 1) * 128], q=q_, mask=None,
                                                nk=128, nq=64, last=(kb == 0), out=q_, kkey="KsT", vkey="Vs", qkey=f"Q{hh}_{8 + ss}"))
                            steps += finish(job, njob)
                            njob += 1
                        run_jobs(steps)

        P.fence()
        dump("modT", modT[:], [128, 6, 16, 3], F32, ["modT"])
        dump("oT", QoT[:, :, 0:256], [128, 8, 256], BF16, [f"Q{hh}_0" for hh in range(8)])
        with ExitStack() as es:
            NWS = 4
            SMALLQ = os.environ.get("KSMALLQ", "sp")
            while conv_todo and EARLY:
                conv_one()
            Ws = [SB(es, f"Ws{i}", [128, 16, 512], BF16) for i in range(NWS)]
            wpl = SB(es, "wpl", [128, 8, 512], BF16)
            xres = SB(es, "xres", [128, ST, D], F32)
            uT = SB(es, "uT2", [128, 16, ST * 128], BF16)
            hT = SB(es, "hT", [128, 44, ST * 128], BF16)
            mT = hT[:, 0:16, :]
            dT = SB(es, "dT", [128, 8, ST * 128], BF16)
            pb = SB(es, "pb", [128, ST, 1024], BF16); pprev = SB(es, "pprev", [32, ST, 1024], BF16)
            p32 = SB(es, "p32", [128, 1024], F32)
            xstg = SB(es, "xstg", [128, D], F32); uTp = SB(es, "uTp", [128, ST, 16, 16], BF16)
            bnd = SB(es, "bnd", [128, 2, 4, 128], BF16); bnp = SB(es, "bnp", [32, 2, 4, 128], BF16)
            sA = SB(es, "sA", [128, 2, ST * 128], F32); sB = SB(es, "sB", [128, 2, ST * 128], F32)
            t1 = SB(es, "t1", [128, 2, ST * 128], F32); t2 = SB(es, "t2", [128, 2, ST * 128], F32)
            tmp = SB(es, "tmp", [128, 1, 512], F32)
            bcp = SB(es, "bcp", [128, 3, 512], F32)
            tb = [PS(es, f"tc{i}", [128, 8, 128], BF16) for i in range(2)]
            B = [PS(es, f"B{i}", [128, 512], F32) for i in range(6)]
            wload(wpl, w_pool, 8, 512, "wpl")
            if os.environ.get("KVERB"):
                print("phase2 sbuf remaining", nc.sbuf_bytes_remaining)

            def wnext(src, nk, ncols, tid):
                i = cnt["w"] % NWS
                cnt["w"] += 1
                sc_ = wscr[tid].rearrange("p (k n) -> p k n", n=512)[:, 0:nk, :]
                if tid not in wcache:
                    wcache[tid] = True
                    wload(Ws[i], src, nk, ncols, f"Ws{i}")
                    P.d("sp", lambda e: e.dma_start(out=sc_, in_=Ws[i][:, 0:nk, :]), reads=[f"Ws{i}"], writes=[f"wscr{tid}"])
                else:
                    P.d("sp", lambda e: e.dma_start(out=Ws[i][:, 0:nk, :], in_=sc_), reads=[f"wscr{tid}"], writes=[f"Ws{i}"])
                return Ws[i], f"Ws{i}"

            def bcast(src_rows, nparts):
                i = cnt["bc"] % 3
                cnt["bc"] += 1
                for (ap_, p0, p1) in src_rows:
                    P.d(SMALLQ, lambda e, ap_=ap_, p0=p0, p1=p1, i=i: e.dma_start(out=bcp[p0:p1, i, :], in_=ap_.partition_broadcast(p1 - p0)),
                        reads=["gscr"], writes=[f"bc{i}"])
                return bcp[:, i, :], f"bc{i}"

            def grows(i_blk, vi, c0):
                if i_blk < 16:
                    return [(gscr[0:1, vi, c0:c0 + 512], 0, 128)]
                return [(gscr[1:2, vi, c0:c0 + 512], 0, 64), (gscr[2:3, vi, c0:c0 + 512], 64, 128)]

            def ln_affine(i_blk, blk, gvec, bvec):
                xa = xres[:, blk, :]
                s, keys = ln_stats(xa, f"xr{blk}", 128)
                P.c("act", lambda e: e.activation(out=xa, in_=xa, func=AF.Identity, bias=nmr[:, s, :], scale=rstd[:, s, :]),
                    reads=keys, writes=[f"xr{blk}"])
                for c4 in range(4):
                    gt, gk = bcast([(gvec[0:1, c4 * 512:(c4 + 1) * 512], 0, 128)], 128)
                    bt, bk = bcast([(bvec[0:1, c4 * 512:(c4 + 1) * 512], 0, 128)], 128)
                    xs_ = xres[:, blk, c4 * 512:(c4 + 1) * 512]
                    P.c("dve", lambda e, xs_=xs_, gt=gt: e.tensor_tensor(out=xs_, in0=xs_, in1=gt, op=ALU.mult), reads=[gk], writes=[f"xr{blk}"])
                    P.c("pool", lambda e, xs_=xs_, bt=bt: e.tensor_tensor(out=xs_, in0=xs_, in1=bt, op=ALU.add), reads=[bk], writes=[f"xr{blk}"])

            tiles = [list(range(a, min(a + ST, 16))) for a in range(0, 16, ST)] + [[16]]
            if "p2" in SKIP:
                tiles = []
            def items_for(blks_):
                it_ = []
                for bi, i in enumerate(blks_):
                    if i < 16:
                        it_.append(("prev", bi, i))
                    it_.append(("main", bi, i))
                return it_

            def prepA(item):
                kind, bi, i = item
                if kind == "prev":
                    P.d("sp", lambda e: e.dma_start(out=xstg[0:16, :], in_=xprev[i * 16:(i + 1) * 16, :]), writes=["xstg"])
                    return lnA(xstg[0:16, :], "xstg", 16)
                P.d("sp", lambda e: e.dma_start(out=xstg[:, :], in_=xown[i * 128:(i + 1) * 128, :]), writes=["xstg"])
                return lnA(xstg[:, :], "xstg", 128)

            def prepB(item, xs):
                kind, bi, i = item
                if kind == "prev":
                    lnB(tb, xs, 16, 1, 0, [(0, 16, 0)], uTp[:, bi], f"uTp{bi}", 0)
                else:
                    lnB(tb, xs, 128, 1, 0, seg_for(i), uT, "uT2", bi * 128)

            if tiles:
                for item in items_for(tiles[0]):
                    prepB(item, prepA(item))
            wp_pre = None
            for ti, blks in enumerate(tiles):
                nxt_items = items_for(tiles[ti + 1]) if ti + 1 < len(tiles) else []
                nb = len(blks)
                T = nb * 128
                tok0 = blks[0] * 128
                wp = wp_pre if wp_pre is not None else [wnext(w_in[:, 3072 + c2 * 512:3072 + (c2 + 1) * 512], 16, 512, c2) for c2 in range(2)]
                for bi, i in enumerate(blks):
                    if i < 16:
                        P.c("pool", lambda e, bi=bi: e.memset(pprev[:, bi, :], 0.0), writes=[f"pprev{bi}"])
                    else:
                        P.c("pool", lambda e: e.memset(p32[0:32, :], 0.0), writes=["p32"])
                        for ss in range(2):
                            P.d(SMALLQ, lambda e, ss=ss: e.dma_start(out=p32[ss * 16:ss * 16 + 15, :], in_=spool[ss]), writes=["p32"])
                        P.c("pool", lambda e, bi=bi: e.tensor_copy(out=pprev[:, bi, :], in_=p32[0:32, :]), reads=["p32"], writes=[f"pprev{bi}"])
                    for c2 in range(2):
                        w_, wk = wp[c2]
                        for dc in range(16):
                            P.c("pe", lambda e, dc=dc, w_=w_, bi=bi: e.matmul(B[0][:, :], lhsT=uT[:, dc, bi * 128:(bi + 1) * 128], rhs=w_[:, dc, :],
                                                                              start=(dc == 0), stop=(dc == 15)), reads=["uT2", wk], writes=["B0"])
                        P.c("act", lambda e, c2=c2: e.copy(out=p32[:, c2 * 512:(c2 + 1) * 512], in_=B[0][:, :]), writes=["B0", "p32"])
                        P.c("dve", lambda e, c2=c2, bi=bi: e.tensor_copy(out=pb[:, bi, c2 * 512:(c2 + 1) * 512], in_=p32[:, c2 * 512:(c2 + 1) * 512]),
                            reads=["p32"], writes=[f"pb{bi}"])
                        if i < 16:
                            for dc in range(16):
                                P.c("pe", lambda e, dc=dc, w_=w_, bi=bi: e.matmul(B[1][0:16, :], lhsT=uTp[:, bi, dc, :], rhs=w_[:, dc, :],
                                                                                  start=(dc == 0), stop=(dc == 15)), reads=[f"uTp{bi}", wk], writes=["B1"])
                            P.c("dve", lambda e, c2=c2, bi=bi: e.tensor_copy(out=pprev[0:16, bi, c2 * 512:(c2 + 1) * 512], in_=B[1][0:16, :]),
                                writes=["B1", f"pprev{bi}"])
                    if i >= 15:
                        P.d(SMALLQ, lambda e, i=i: e.dma_start(out=pout[i - 15], in_=p32[:]), reads=["p32"])
                    bs = i % 2
                    P.d(SMALLQ, lambda e, i=i, bs=bs: e.dma_start(out=bnd[:, bs], in_=bands[i, 0:128]), writes=[f"bnd{bs}"])
                    P.d(SMALLQ, lambda e, i=i, bs=bs: e.dma_start(out=bnp[:, bs], in_=bands[i, 128:160]), writes=[f"bnd{bs}"])
                    for half in range(2):
                        for c4 in range(4):
                            cc = half * 4 + c4
                            g = cc // 2
                            P.c("pe", lambda e, cc=cc, g=g, bi=bi, bs=bs, half=half, c4=c4: e.matmul(
                                B[2 + half][:, c4 * 128:(c4 + 1) * 128], lhsT=pb[:, bi, cc * 128:(cc + 1) * 128], rhs=bnd[:, bs, g, :],
                                start=True, stop=False), reads=[f"pb{bi}", f"bnd{bs}"], writes=[f"B{2 + half}"])
                            P.c("pe", lambda e, cc=cc, g=g, bi=bi, bs=bs, half=half, c4=c4: e.matmul(
                                B[2 + half][:, c4 * 128:(c4 + 1) * 128], lhsT=pprev[:, bi, cc * 128:(cc + 1) * 128], rhs=bnp[:, bs, g, :],
                                start=False, stop=True), reads=[f"pprev{bi}", f"bnd{bs}"], writes=[f"B{2 + half}"])
                        P.c("dve", lambda e, half=half, bi=bi: e.tensor_copy(
                            out=dT[:, half * 4:half * 4 + 4, bi * 128:(bi + 1) * 128],
                            in_=B[2 + half][:, :].rearrange("p (a b) -> p a b", a=4)), writes=[f"B{2 + half}", "dT"])
                qkeys = sorted(set(f"Q{hh}_{(i // 2) if i < 16 else 8 + ss}" for hh in range(8) for i in blks
                                   for ss in ((0,) if i < 16 else (0, 1))))
                for c4 in range(4):
                    wA, kA = wnext(w_in[:, 4096 + c4 * 512:4096 + (c4 + 1) * 512], 16, 512, 2 + c4)
                    wB, kB = wnext(w_in[:, 6144 + c4 * 512:6144 + (c4 + 1) * 512], 16, 512, 6 + c4)
                    wS, kS = wnext(w_sb[:, c4 * 512:(c4 + 1) * 512], 8, 512, 10 + c4)
                    for fs in range(4):
                        fc = c4 * 4 + fs
                        s2 = fc % 2
                        fsl = slice(fs * 128, (fs + 1) * 128)
                        for dc in range(16):
                            P.c("pe", lambda e, dc=dc, wA=wA, fsl=fsl: e.matmul(B[0][:, 0:T], lhsT=wA[:, dc, fsl], rhs=uT[:, dc, 0:T],
                                                                               start=(dc == 0), stop=(dc == 15)), reads=["uT2", kA], writes=["B0"])
                        for dc in range(16):
                            P.c("pe", lambda e, dc=dc, wB=wB, fsl=fsl: e.matmul(B[1][:, 0:T], lhsT=wB[:, dc, fsl], rhs=uT[:, dc, 0:T],
                                                                               start=(dc == 0), stop=(dc == 15)), reads=["uT2", kB], writes=["B1"])
                        for h in range(8):
                            P.c("pe", lambda e, h=h, wS=wS, fsl=fsl: e.matmul(B[2][:, 0:T], lhsT=wS[:, h, fsl], rhs=QoT[:, h, tok0:tok0 + T],
                                                                             start=(h == 0), stop=(h == 7)), reads=qkeys + [kS], writes=["B2"])
                        for j2 in range(2):
                            P.c("pe", lambda e, j2=j2, c4=c4, fsl=fsl: e.matmul(B[3][:, 0:T], lhsT=wpl[:, 2 * c4 + j2, fsl], rhs=dT[:, 2 * c4 + j2, 0:T],
                                                                               start=(j2 == 0), stop=(j2 == 1)), reads=["dT", "wpl"], writes=["B3"])
                        P.c("act", lambda e, s2=s2: e.activation(out=sA[:, s2, 0:T], in_=B[0][:, 0:T], func=AF.Sigmoid), writes=["B0", f"sA{s2}"])
                        P.c("act", lambda e, s2=s2: e.activation(out=sB[:, s2, 0:T], in_=B[1][:, 0:T], func=AF.Sigmoid), writes=["B1", f"sB{s2}"])
                        P.c("dve", lambda e, s2=s2: e.tensor_tensor(out=t1[:, s2, 0:T], in0=B[2][:, 0:T], in1=sA[:, s2, 0:T], op=ALU.mult),
                            reads=[f"sA{s2}"], writes=["B2", f"t1{s2}"])
                        P.c("dve", lambda e, s2=s2, fc=fc: e.scalar_tensor_tensor(out=t2[:, s2, 0:T], in0=B[3][:, 0:T], scalar=pst[:, fc:fc + 1],
                                                                                  in1=sB[:, s2, 0:T], op0=ALU.mult, op1=ALU.mult),
                            reads=[f"sB{s2}", "pst"], writes=["B3", f"t2{s2}"])
                        P.c("pool", lambda e, s2=s2, fc=fc: e.tensor_tensor(out=mT[:, fc, 0:T], in0=t1[:, s2, 0:T], in1=t2[:, s2, 0:T], op=ALU.add),
                            reads=[f"t1{s2}", f"t2{s2}"], writes=["mT", "hT"])
                if blks[0] == 0:
                    dump("pb", pb[:], [128, ST, 1024], BF16, ["pb0", "pb1"])
                    dump("pprev", pprev[:], [32, ST, 1024], BF16, ["pprev0", "pprev1"])
                    dump("dT", dT[:, :, 0:256], [128, 8, 256], BF16, ["dT"])
                    dump("mT", mT[:, :, 0:256], [128, 16, 256], BF16, ["mT"])
                for bi, i in enumerate(blks):
                    P.d(SMALLQ, lambda e, i=i, bi=bi: e.dma_start(out=xres[:, bi, :], in_=xown[i * 128:(i + 1) * 128, :]), writes=[f"xr{bi}"])
                for c4 in range(4):
                    wO, kO = wnext(w_out[:, c4 * 512:(c4 + 1) * 512], 16, 512, 14 + c4)
                    for bi, i in enumerate(blks):
                        bb = 4 + (bi % 2)
                        for dc in range(16):
                            P.c("pe", lambda e, dc=dc, wO=wO, bi=bi, bb=bb: e.matmul(B[bb][:, :], lhsT=mT[:, dc, bi * 128:(bi + 1) * 128], rhs=wO[:, dc, :],
                                                                                    start=(dc == 0), stop=(dc == 15)), reads=["mT", kO], writes=[f"B{bb}"])
                        gt, gk = bcast(grows(i, 0, c4 * 512), 128)
                        ts = 0
                        P.c("dve", lambda e, bb=bb, gt=gt, ts=ts: e.tensor_tensor(out=tmp[:, ts, :], in0=B[bb][:, :], in1=gt, op=ALU.mult),
                            reads=[gk], writes=[f"B{bb}", f"tmp{ts}"])
                        xs_ = xres[:, bi, c4 * 512:(c4 + 1) * 512]
                        P.c("dve", lambda e, xs_=xs_, ts=ts: e.scalar_tensor_tensor(out=xs_, in0=xs_, scalar=ALPHA, in1=tmp[:, ts, :],
                                                                                    op0=ALU.mult, op1=ALU.add), reads=[f"tmp{ts}"], writes=[f"xr{bi}"])
                if blks[0] == 0:
                    dump("r1", xres[:, 0, :], [128, D], F32, ["xr0"])
                ffn_pre = (wnext(w_gate[:, 0:512], 16, 512, 18), wnext(w_up[:, 0:512], 16, 512, 29))
                for bi, i in enumerate(blks):
                    ln_affine(i, bi, ln1g, ln1b)
                xs26 = [lnA(xres[:, bi, :], f"xr{bi}", 128) for bi, i in enumerate(blks)]
                for bi, i in enumerate(blks):
                    lnB(tb, xs26[bi], 128, 4, 3, seg_for(i), uT, "uT2", bi * 128)
                if blks[0] == 0:
                    dump("x1", xres[:, 0, :], [128, D], F32, ["xr0"])
                    dump("u2T", uT[:, :, 0:256], [128, 16, 256], BF16, ["uT2"])
                for c11 in range(11):
                    if c11 == 0:
                        (wG, kG), (wU, kU) = ffn_pre
                    else:
                        wG, kG = wnext(w_gate[:, c11 * 512:(c11 + 1) * 512], 16, 512, 18 + c11)
                        wU, kU = wnext(w_up[:, c11 * 512:(c11 + 1) * 512], 16, 512, 29 + c11)
                    for fs in range(4):
                        ffc = c11 * 4 + fs
                        s2 = ffc % 2
                        fsl = slice(fs * 128, (fs + 1) * 128)
                        for dc in range(16):
                            P.c("pe", lambda e, dc=dc, wG=wG, fsl=fsl, s2=s2: e.matmul(B[s2][:, 0:T], lhsT=wG[:, dc, fsl], rhs=uT[:, dc, 0:T],
                                                                                      start=(dc == 0), stop=(dc == 15)), reads=["uT2", kG], writes=[f"B{s2}"])
                        for dc in range(16):
                            P.c("pe", lambda e, dc=dc, wU=wU, fsl=fsl, s2=s2: e.matmul(B[2 + s2][:, 0:T], lhsT=wU[:, dc, fsl], rhs=uT[:, dc, 0:T],
                                                                                      start=(dc == 0), stop=(dc == 15)), reads=["uT2", kU], writes=[f"B{2 + s2}"])
                        P.c("act", lambda e, s2=s2: e.activation(out=sA[:, s2, 0:T], in_=B[s2][:, 0:T], func=AF.Silu), writes=[f"B{s2}", f"sA{s2}"])
                        P.c("dve", lambda e, s2=s2, ffc=ffc: e.tensor_tensor(out=hT[:, ffc, 0:T], in0=B[2 + s2][:, 0:T], in1=sA[:, s2, 0:T], op=ALU.mult),
                            reads=[f"sA{s2}"], writes=[f"B{2 + s2}", "hT", "mT"])
                if blks[0] == 0:
                    dump("hT", hT[:, :, 0:256], [128, 44, 256], BF16, ["hT"])
                nxs = {}
                if nxt_items:
                    nxs[0] = prepA(nxt_items[0])
                for c4 in range(4):
                    wd = [wnext(w_down[pc_ * 2048:min((pc_ + 1) * 2048, DFF), c4 * 512:(c4 + 1) * 512], 16 if pc_ < 2 else 12, 512, 40 + c4 * 3 + pc_)
                          for pc_ in range(3)]
                    for bi, i in enumerate(blks):
                        bb = 4 + (bi % 2)
                        for ffc in range(44):
                            w_, wk = wd[ffc // 16]
                            P.c("pe", lambda e, ffc=ffc, w_=w_, bi=bi, bb=bb: e.matmul(B[bb][:, :], lhsT=hT[:, ffc, bi * 128:(bi + 1) * 128],
                                                                                      rhs=w_[:, ffc % 16, :], start=(ffc == 0), stop=(ffc == 43)),
                                reads=["hT", wk], writes=[f"B{bb}"])
                        gt, gk = bcast(grows(i, 1, c4 * 512), 128)
                        ts = 0
                        P.c("dve", lambda e, bb=bb, gt=gt, ts=ts: e.tensor_tensor(out=tmp[:, ts, :], in0=B[bb][:, :], in1=gt, op=ALU.mult),
                            reads=[gk], writes=[f"B{bb}", f"tmp{ts}"])
                        xs_ = xres[:, bi, c4 * 512:(c4 + 1) * 512]
                        P.c("dve", lambda e, xs_=xs_, ts=ts: e.scalar_tensor_tensor(out=xs_, in0=xs_, scalar=ALPHA, in1=tmp[:, ts, :],
                                                                                    op0=ALU.mult, op1=ALU.add), reads=[f"tmp{ts}"], writes=[f"xr{bi}"])
                    if c4 < len(nxt_items):
                        prepB(nxt_items[c4], nxs[c4])
                        if c4 + 1 < len(nxt_items):
                            nxs[c4 + 1] = prepA(nxt_items[c4 + 1])
                if blks[0] == 0:
                    dump("r2", xres[:, 0, :], [128, D], F32, ["xr0"])
                wp_pre = [wnext(w_in[:, 3072 + c2 * 512:3072 + (c2 + 1) * 512], 16, 512, c2) for c2 in range(2)] if nxt_items else None
                for bi, i in enumerate(blks):
                    ln_affine(i, bi, ln2g, ln2b)
                    P.d(SMALLQ, lambda e, i=i, bi=bi: e.dma_start(out=yown[i * 128:(i + 1) * 128, :], in_=xres[:, bi, :]), reads=[f"xr{bi}"])
        P.emit()
    return nc


def _own_blocks(par):
    if par == 0:
        return [4 * j + e for j in range(8) for e in (0, 3)]
    return [4 * j + e for j in range(8) for e in (1, 2)]


def _consts(par):
    bf = ml_dtypes.bfloat16
    own = _own_blocks(par)
    qoff = [0, 3] if par == 0 else [1, 2]
    s_ = np.arange(128)[:, None]
    t_ = np.arange(128)[None, :]
    maskp = np.zeros((128, 4, 256), np.float32)
    for r in range(4):
        for e in range(2):
            maskp[:, r, e * 128:(e + 1) * 128] = ((r * 128 + s_) < (qoff[e] * 128 + t_)).astype(np.float32)
    masks = (np.arange(64)[:, None] < np.arange(64)[None, :]).astype(np.float32)
    tri = (s_ >= t_).astype(np.float32)
    bands = np.zeros((NOWN, 160, 4, 128), np.float32)
    wins = (2, 4, 8, 16)
    for i in range(NOWN):
        for g, win in enumerate(wins):
            if i < 16:
                gb = own[i]
                pos = gb * 128 + np.arange(128)
                cntv = np.minimum(win, pos + 1).astype(np.float32)
                src = gb * 128 + np.arange(128)
                m = (src[:, None] > pos[None, :] - win) & (src[:, None] <= pos[None, :])
                bands[i, 0:128, g, :] = m / cntv[None, :] - np.eye(128)
                if gb > 0:
                    srcp = gb * 128 - 16 + np.arange(16)
                    mp = (srcp[:, None] > pos[None, :] - win)
                    bands[i, 128:144, g, :] = mp / cntv[None, :]
            else:
                tt = np.arange(128) % 64
                sq = np.arange(128) // 64
                m = (sq[:, None] == sq[None, :]) & (tt[:, None] > tt[None, :] - win) & (tt[:, None] <= tt[None, :])
                bands[i, 0:128, g, :] = m / float(win) - np.eye(128)
                for ss in range(2):
                    rel = np.arange(15) - 15
                    mp = (rel[:, None] > tt[None, :] - win) & (sq[None, :] == ss)
                    bands[i, 128 + ss * 16:128 + ss * 16 + 15, g, :] = mp / float(win)
    return dict(maskp=maskp.astype(bf), masks=masks.astype(bf), tri_b=tri.astype(bf),
                ones_b=np.ones((128, 128), bf), ident_b=np.eye(128).astype(bf),
                ident_f=np.eye(128, dtype=np.float32), bands=bands.astype(bf))


_NC_CACHE = {}


def kernel(x_prompt, x_sample, c_prompt, c_sample, cache_k, cache_v, state_pool, w_ada, b_ada, w_in,
           w_sb_out, w_pool, pool_scale, w_out, ln1_g, ln1_b, w_gate, w_up, w_down, ln2_g, ln2_b):
    f = lambda a: np.ascontiguousarray(np.asarray(a, dtype=np.float32))
    x_prompt, x_sample, c_prompt, c_sample = f(x_prompt), f(x_sample), f(c_prompt), f(c_sample)
    cache_k, cache_v, state_pool = f(cache_k), f(cache_v), f(state_pool)
    shared = dict(
        w_ada=f(w_ada[0]), badaT=f(np.asarray(b_ada[0]).reshape(96, 128).T), bada=f(np.asarray(b_ada[0]).reshape(1, -1)),
        w_in=f(w_in[0]), w_sb=f(w_sb_out[0]), w_pool=f(np.asarray(w_pool[0]).reshape(1024, 512)),
        psT=f(np.asarray(pool_scale[0]).reshape(16, 128).T), w_out=f(w_out[0]),
        ln1g=f(np.asarray(ln1_g[0]).reshape(1, -1)), ln1b=f(np.asarray(ln1_b[0]).reshape(1, -1)),
        ln2g=f(np.asarray(ln2_g[0]).reshape(1, -1)), ln2b=f(np.asarray(ln2_b[0]).reshape(1, -1)),
        w_gate=f(w_gate[0]), w_up=f(w_up[0]), w_down=f(w_down[0]))
    consts = [_consts(0), _consts(1)]
    in_maps = []
    for c in range(8):
        b, par = c // 2, c % 2
        own = _own_blocks(par)
        xs = x_prompt[b]
        xown = np.concatenate([xs[g * 128:(g + 1) * 128] for g in own] + [x_sample[2 * c], x_sample[2 * c + 1]], axis=0)
        xprev = np.zeros((NOWN * 16, D), np.float32)
        for i, g in enumerate(own):
            if g > 0:
                xprev[i * 16:(i + 1) * 16] = xs[g * 128 - 16:g * 128]
        c3 = np.stack([c_prompt[b], c_sample[2 * c], c_sample[2 * c + 1]], axis=0)
        cT = np.ascontiguousarray(c3.reshape(3, 16, 128).transpose(2, 1, 0))
        m = dict(xseq=xs, xown=np.ascontiguousarray(xown), xprev=xprev, cT=cT,
                 ck=np.ascontiguousarray(cache_k[0, 2 * c:2 * c + 2]), cv=np.ascontiguousarray(cache_v[0, 2 * c:2 * c + 2]),
                 spool=np.ascontiguousarray(state_pool[0, 2 * c:2 * c + 2]))
        m.update(shared)
        m.update(consts[par])
        in_maps.append(m)
    if "nc" not in _NC_CACHE:
        _NC_CACHE["nc"] = build_nc()
    res = run_bass_kernel_spmd(_NC_CACHE["nc"], in_maps, core_ids=list(range(8)))
    R = res.results
    if DEBUG:
        DBG_OUT.clear()
        DBG_OUT.update({k: v for k, v in R[0].items() if k.startswith("dbg_")})
    y_prompt = np.zeros((4, 4096, D), np.float32); y_sample = np.zeros((16, 64, D), np.float32)
    k_prompt = np.zeros((1, 4, 8, 4096, 128), np.float32); v_prompt = np.zeros_like(k_prompt)
    k_sample = np.zeros((1, 16, 8, 64, 128), np.float32); v_sample = np.zeros_like(k_sample)
    pool_prompt = np.zeros((1, 4, 15, 1024), np.float32); pool_sample = np.zeros((1, 16, 15, 1024), np.float32)
    for c in range(8):
        b, par = c // 2, c % 2
        own = _own_blocks(par)
        yo = R[c]["yown"]
        for i, g in enumerate(own):
            y_prompt[b, g * 128:(g + 1) * 128] = yo[i * 128:(i + 1) * 128]
        for ss in range(2):
            y_sample[2 * c + ss] = yo[2048 + ss * 64:2048 + (ss + 1) * 64]
            k_sample[0, 2 * c + ss] = R[c]["ksmp"][ss * 64:(ss + 1) * 64].reshape(64, 8, 128).transpose(1, 0, 2)
            v_sample[0, 2 * c + ss] = R[c]["vsmp"][ss * 64:(ss + 1) * 64].reshape(64, 8, 128).transpose(1, 0, 2)
            pool_sample[0, 2 * c + ss] = R[c]["pout"][1][ss * 64 + 49:ss * 64 + 64]
        if par == 0:
            k_prompt[0, b] = R[c]["kseq"].reshape(4096, 8, 128).transpose(1, 0, 2)
            v_prompt[0, b] = R[c]["vseq"].reshape(4096, 8, 128).transpose(1, 0, 2)
            pool_prompt[0, b] = R[c]["pout"][0][113:128]
    return (y_prompt, y_sample, k_prompt, v_prompt, pool_prompt, k_sample, v_sample, pool_sample)
```

```python
import numpy as np
from contextlib import ExitStack
import ml_dtypes
import concourse.bass as bass
import concourse.mybir as mybir
from concourse.bass_utils import run_bass_kernel_spmd

F32 = mybir.dt.float32
BF16 = mybir.dt.bfloat16
AF = mybir.ActivationFunctionType
ALU = mybir.AluOpType
ENGS = ("pe", "act", "dve", "pool", "sp")
D = 2048
DFF = 5632
ALPHA = 2.0 ** 0.25
SCALE = 128.0 ** -0.5
NOWN = 17
ST = 2
DEBUG = False
import os
SKIP = os.environ.get("KSKIP", "")
DBG_OUT = {}


import types


def _snap(fn):
    if fn.__closure__ is None:
        return fn
    cells = []
    for c_ in fn.__closure__:
        try:
            cells.append(types.CellType(c_.cell_contents))
        except ValueError:
            cells.append(c_)
    return types.FunctionType(fn.__code__, fn.__globals__, fn.__name__, fn.__defaults__, tuple(cells))


class Op:
    __slots__ = ("eng", "fn", "kind", "deps", "signal", "sigval", "sem", "idx", "prewait")

    def __init__(self, eng, fn, kind):
        self.eng, self.fn, self.kind = eng, _snap(fn), kind
        self.deps = []
        self.signal = False
        self.sigval = 0
        self.sem = None
        self.prewait = None


class Prog:
    NDMASEM = 12

    def __init__(self, nc):
        self.nc = nc
        self.ops = {e: [] for e in ENGS}
        self.last_w = {}
        self.readers = {}
        self.ndma = {e: 0 for e in ENGS}
        self.all_dma = []
        self._fence_dma = 0

    def _add(self, op, reads, writes):
        deps = []
        for r in reads:
            w = self.last_w.get(r)
            if w is not None:
                deps.append((w, True))
        for w_ in writes:
            w = self.last_w.get(w_)
            if w is not None:
                deps.append((w, False))
            deps.extend((x, False) for x in self.readers.get(w_, ()))
        seen = set()
        for d, raw in deps:
            if d is op or id(d) in seen:
                continue
            if d.eng == op.eng and d.kind == "c" and op.kind == "c" and not raw:
                continue
            seen.add(id(d))
            op.deps.append(d)
        for r in reads:
            self.readers.setdefault(r, []).append(op)
        for w_ in writes:
            self.last_w[w_] = op
            self.readers[w_] = []
        self.ops[op.eng].append(op)
        return op

    def c(self, eng, fn, reads=(), writes=()):
        return self._add(Op(eng, fn, "c"), reads, writes)

    def d(self, eng, fn, reads=(), writes=()):
        op = Op(eng, fn, "d")
        op.idx = self.ndma[eng]
        self.ndma[eng] += 1
        self.all_dma.append(op)
        return self._add(op, reads, writes)

    def fence(self):
        lasts = []
        for e in ENGS:
            cs = [o for o in self.ops[e] if o.kind == "c"]
            if cs:
                lasts.append(cs[-1])
        dmas = list(self.all_dma[self._fence_dma:])
        self._fence_dma = len(self.all_dma)
        for e in ENGS:
            op = Op(e, (lambda eng: eng.nop()), "c")
            op.deps = [d for d in lasts + dmas]
            self.ops[e].append(op)

    def emit(self):
        nc = self.nc
        for e in ENGS:
            for op in self.ops[e]:
                for d in op.deps:
                    if d.kind == "c":
                        if d.eng == op.eng and op.kind == "c" and e == "pe":
                            continue
                        d.signal = True
        csem = {e: nc.alloc_semaphore(name=f"c_{e}") for e in ENGS}
        dsem = {e: [nc.alloc_semaphore(name=f"d_{e}_{i}") for i in range(self.NDMASEM)]
                for e in ENGS if self.ndma[e] > 0}
        P = self.NDMASEM
        for e in ENGS:
            cnt = 0
            for op in self.ops[e]:
                if op.kind == "c":
                    if op.signal:
                        cnt += 1
                        op.sigval = cnt
                        op.sem = csem[e]
                else:
                    op.sem = dsem[e][op.idx % P]
                    op.sigval = 16 * (op.idx // P + 1)
                    if op.idx >= P:
                        op.prewait = (op.sem, 16 * (op.idx // P))
        final_dma = {}
        for op in self.all_dma:
            final_dma[id(op.sem)] = (op.sem, max(op.sigval, final_dma.get(id(op.sem), (None, 0))[1]))
        with nc.Block() as block:
            def run(e):
                def body(eng):
                    waited = {}

                    def wait(sem, val):
                        k = id(sem)
                        if waited.get(k, 0) >= val:
                            return
                        waited[k] = val
                        eng.wait_ge(sem, val)

                    for op in self.ops[e]:
                        need = {}
                        if op.prewait is not None:
                            need[id(op.prewait[0])] = op.prewait
                        for d in op.deps:
                            if d.kind == "c" and not d.signal:
                                continue
                            k = id(d.sem)
                            if k not in need or need[k][1] < d.sigval:
                                need[k] = (d.sem, d.sigval)
                        for sem, val in need.values():
                            wait(sem, val)
                        ins = op.fn(eng)
                        if op.kind == "d":
                            ins.then_inc(op.sem, 16)
                        elif op.signal:
                            ins.then_inc(op.sem, 1)
                    if e == "sp":
                        for sem, val in final_dma.values():
                            wait(sem, val)
                return body
            block.tensor(run("pe"))
            block.scalar(run("act"))
            block.vector(run("dve"))
            block.gpsimd(run("pool"))
            block.sync(run("sp"))


def build_nc():
    nc = bass.Bass("TRN2", target_bir_lowering=False)

    def din(name, shape, dt=F32):
        return nc.dram_tensor(name, list(shape), dt, kind="ExternalInput").ap()

    def dout(name, shape):
        return nc.dram_tensor(name, list(shape), F32, kind="ExternalOutput").ap()

    xseq = din("xseq", [4096, D]); xown = din("xown", [NOWN * 128, D]); xprev = din("xprev", [NOWN * 16, D])
    cT = din("cT", [128, 16, 3]); ck = din("ck", [2, 8, 1024, 128]); cv = din("cv", [2, 8, 1024, 128])
    spool = din("spool", [2, 15, 1024])
    w_ada = din("w_ada", [D, 6 * D]); badaT = din("badaT", [128, 96]); bada = din("bada", [1, 6 * D])
    w_in = din("w_in", [D, 8192]); w_sb = din("w_sb", [1024, D]); w_pool = din("w_pool", [1024, 512])
    psT = din("psT", [128, 16]); w_out = din("w_out", [D, D])
    ln1g = din("ln1g", [1, D]); ln1b = din("ln1b", [1, D]); ln2g = din("ln2g", [1, D]); ln2b = din("ln2b", [1, D])
    w_gate = din("w_gate", [D, DFF]); w_up = din("w_up", [D, DFF]); w_down = din("w_down", [DFF, D])
    ident_b = din("ident_b", [128, 128], BF16); ident_f = din("ident_f", [128, 128])
    tri_b = din("tri_b", [128, 128], BF16); ones_b = din("ones_b", [128, 128], BF16)
    maskp = din("maskp", [128, 4, 256], BF16); masks = din("masks", [64, 64], BF16)
    bands = din("bands", [NOWN, 160, 4, 128], BF16)
    yown = dout("yown", [NOWN * 128, D]); kseq = dout("kseq", [4096, 1024]); vseq = dout("vseq", [4096, 1024])
    ksmp = dout("ksmp", [128, 1024]); vsmp = dout("vsmp", [128, 1024]); pout = dout("pout", [2, 128, 1024])
    gscr = nc.dram_tensor("gscr", [3, 2, D], F32).ap()
    NWT = 52
    wscr = nc.dram_tensor("wscr", [NWT, 128, 16 * 512], BF16).ap()

    P = Prog(nc)
    dbg_names = []

    def dump(name, ap_, shape, dt, reads):
        if not DEBUG:
            return
        t_ = nc.dram_tensor("dbg_" + name, list(shape), dt, kind="ExternalOutput").ap()
        dbg_names.append("dbg_" + name)
        P.d("sp", lambda e: e.dma_start(out=t_, in_=ap_), reads=reads)

    with ExitStack() as top:
        uid = [0]

        def SB(es, name, shape, dt):
            uid[0] += 1
            return es.enter_context(nc.sbuf_tensor(f"{name}_{uid[0]}", list(shape), dt))

        def PS(es, name, shape, dt):
            uid[0] += 1
            return es.enter_context(nc.psum_tensor(f"{name}_{uid[0]}", list(shape), dt))

        idb = SB(top, "idb", [128, 128], BF16); idf = SB(top, "idf", [128, 128], F32)
        tri = SB(top, "tri", [128, 128], BF16); ones = SB(top, "ones", [128, 128], BF16)
        modT = SB(top, "modT", [128, 6, 16, 3], F32)
        pst = SB(top, "pst", [128, 16], F32)
        mhalf = SB(top, "mhalf", [128, 1], F32)
        QoT = SB(top, "QoT", [128, 8, NOWN * 128], BF16)
        stats = SB(top, "stats", [128, 2, 4, 6], F32); mv = SB(top, "mv", [128, 2, 2], F32)
        veps = SB(top, "veps", [128, 2, 1], F32); rstd = SB(top, "rstd", [128, 2, 1], F32)
        nmr = SB(top, "nmr", [128, 2, 1], F32)
        xn = SB(top, "xn", [128, 2, D], BF16)
        for t_, src, key_ in ((idb, ident_b, "idb"), (idf, ident_f, "idf"), (tri, tri_b, "tri"), (ones, ones_b, "ones"),
                              (pst, psT, "pst")):
            P.d("sp", lambda e, t_=t_, src=src: e.dma_start(out=t_[:], in_=src), writes=[key_])
        P.c("pool", lambda e: e.memset(mhalf[:], -0.5), writes=["mhalf"])
        cnt = {"st": 0, "w": 0, "bc": 0, "xn": 0}

        def ln_stats(xa, xkey, n):
            s = cnt["st"] % 2
            cnt["st"] += 1
            k_ = f"st{s}"
            for q in range(4):
                P.c("dve", lambda e, q=q: e.bn_stats(out=stats[0:n, s, q, :], in_=xa[:, q * 512:(q + 1) * 512]),
                    reads=[xkey], writes=[k_ + "a"])
            P.c("dve", lambda e: e.bn_aggr(out=mv[0:n, s, :], in_=stats[0:n, s, :, :]), reads=[k_ + "a"], writes=[k_ + "b"])
            P.c("dve", lambda e: e.tensor_scalar_add(veps[0:n, s, :], mv[0:n, s, 1:2], 1e-5), reads=[k_ + "b"], writes=[k_ + "c"])
            P.c("pool", lambda e: e.tensor_tensor(out=rstd[0:n, s, :], in0=veps[0:n, s, :], in1=mhalf[0:n, :], op=ALU.pow),
                reads=[k_ + "c", "mhalf"], writes=[k_ + "d"])
            P.c("dve", lambda e: e.scalar_tensor_tensor(out=nmr[0:n, s, :], in0=mv[0:n, s, 0:1], scalar=-1.0,
                                                        in1=rstd[0:n, s, :], op0=ALU.mult, op1=ALU.mult),
                reads=[k_ + "b", k_ + "d"], writes=[k_ + "e"])
            return s, [k_ + "d", k_ + "e"]

        def lnA(xa, xkey, n):
            s, keys = ln_stats(xa, xkey, n)
            xs = cnt["xn"] % 2
            cnt["xn"] += 1
            P.c("act", lambda e: e.activation(out=xn[0:n, xs, :], in_=xa, func=AF.Identity,
                                              bias=nmr[0:n, s, :], scale=rstd[0:n, s, :]),
                reads=[xkey] + keys, writes=[f"xn{xs}"])
            return xs

        def lnB(tb, xs, n, vsc, vsh, segs, dst, dkey, doff):
            for k in range(16):
                b_ = k // 8
                P.c("pe", lambda e, k=k, b_=b_: e.transpose(out=tb[b_][:, k % 8, 0:n], in_=xn[0:n, xs, k * 128:(k + 1) * 128],
                                                            identity=idb[0:n, 0:n]),
                    reads=[f"xn{xs}", "idb"], writes=[f"TB{b_}"])
            for k in range(16):
                b_ = k // 8
                for (lo, hi, sq) in segs:
                    if b_ == 0:
                        P.c("act", lambda e, k=k, lo=lo, hi=hi, sq=sq: e.activation(
                            out=dst[:, k, doff + lo:doff + hi], in_=tb[0][:, k % 8, lo:hi], func=AF.Identity,
                            bias=modT[:, vsh, k, sq:sq + 1], scale=modT[:, vsc, k, sq:sq + 1]),
                            reads=["modT"], writes=["TB0", dkey])
                    else:
                        P.c("dve", lambda e, k=k, lo=lo, hi=hi, sq=sq: e.tensor_scalar(
                            out=dst[:, k, doff + lo:doff + hi], in0=tb[1][:, k % 8, lo:hi],
                            scalar1=modT[:, vsc, k, sq:sq + 1], scalar2=modT[:, vsh, k, sq:sq + 1],
                            op0=ALU.mult, op1=ALU.add),
                            reads=["modT"], writes=["TB1", dkey])

        def ln0(tb, xa, xkey, n, vsc, vsh, segs, dst, dkey, doff):
            xs = lnA(xa, xkey, n)
            lnB(tb, xs, n, vsc, vsh, segs, dst, dkey, doff)

        def wload(wt, src, nk, ncols, key):
            P.d("pool", lambda e: e.dma_start(out=wt[:, 0:nk, 0:ncols], in_=src.rearrange("(dc p) n -> p dc n", p=128)),
                writes=[key])

        def seg_for(i):
            return [(0, 128, 0)] if i < 16 else [(0, 64, 1), (64, 128, 2)]

        wtiles = {}
        for c2 in range(2):
            wtiles[c2] = (w_in[:, 3072 + c2 * 512:3072 + (c2 + 1) * 512], 16)
        for c4 in range(4):
            wtiles[2 + c4] = (w_in[:, 4096 + c4 * 512:4096 + (c4 + 1) * 512], 16)
            wtiles[6 + c4] = (w_in[:, 6144 + c4 * 512:6144 + (c4 + 1) * 512], 16)
            wtiles[10 + c4] = (w_sb[:, c4 * 512:(c4 + 1) * 512], 8)
            wtiles[14 + c4] = (w_out[:, c4 * 512:(c4 + 1) * 512], 16)
            for pc_ in range(3):
                wtiles[40 + c4 * 3 + pc_] = (w_down[pc_ * 2048:min((pc_ + 1) * 2048, DFF), c4 * 512:(c4 + 1) * 512], 16 if pc_ < 2 else 12)
        for c11 in range(11):
            wtiles[18 + c11] = (w_gate[:, c11 * 512:(c11 + 1) * 512], 16)
            wtiles[29 + c11] = (w_up[:, c11 * 512:(c11 + 1) * 512], 16)
        conv_todo = sorted(wtiles.keys())
        wcache = {}
        EARLY = os.environ.get("KEARLY", "1") == "1"

        def conv_one():
            if not EARLY or not conv_todo:
                return
            tid = conv_todo.pop(0)
            src, nk = wtiles[tid]
            dst = wscr[tid].rearrange("p (k n) -> p k n", n=512)[:, 0:nk, :]
            P.d("pool", lambda e: e.dma_start(out=dst, in_=src.rearrange("(dc p) n -> p dc n", p=128)), writes=[f"wscr{tid}"])
            wcache[tid] = True

        with ExitStack() as es:
            ct32 = SB(es, "ct32", [128, 16, 3], F32); cex = SB(es, "cex", [128, 16, 3], F32)
            scT = SB(es, "scT", [128, 16, 3], BF16)
            bT = SB(es, "bT", [128, 96], F32); brow = SB(es, "brow", [3, 6 * D], F32)
            wa = [SB(es, f"wa{i}", [128, 16, 512], BF16) for i in range(2)]
            grow = SB(es, "grow", [3, 2, 512], F32)
            pg = PS(es, "pg", [128, 512], F32); pm = PS(es, "pm", [128, 6, 16, 3], F32)
            P.d("sp", lambda e: e.dma_start(out=ct32[:], in_=cT), writes=["ct32"])
            P.d("sp", lambda e: e.dma_start(out=bT[:], in_=badaT), writes=["bT"])
            P.d("sp", lambda e: e.dma_start(out=brow[:], in_=bada.partition_broadcast(3)), writes=["brow"])
            P.c("act", lambda e: e.activation(out=cex[:], in_=ct32[:], func=AF.Exp, scale=-1.0), reads=["ct32"], writes=["cex"])
            P.c("dve", lambda e: e.tensor_scalar_add(cex[:], cex[:], 1.0), reads=["cex"], writes=["cex"])
            P.c("dve", lambda e: e.reciprocal(cex[:], cex[:]), reads=["cex"], writes=["cex"])
            P.c("dve", lambda e: e.tensor_tensor(out=scT[:], in0=ct32[:], in1=cex[:], op=ALU.mult), reads=["cex", "ct32"], writes=["scT"])
            for ct in range(24):
                v, q4 = ct // 4, ct % 4
                w_ = wa[ct % 2]
                wk = f"wa{ct % 2}"
                wload(w_, w_ada[:, ct * 512:(ct + 1) * 512], 16, 512, wk)
                if v in (2, 5):
                    for dc in range(16):
                        P.c("pe", lambda e, dc=dc, w_=w_: e.matmul(pg[0:3, :], lhsT=scT[:, dc, :], rhs=w_[:, dc, :],
                                                                   start=(dc == 0), stop=(dc == 15)),
                            reads=[wk, "scT"], writes=["PG"])
                    vi = 0 if v == 2 else 1
                    P.c("dve", lambda e, vi=vi, ct=ct: e.tensor_tensor(out=grow[:, vi, :], in0=pg[0:3, :],
                                                                       in1=brow[:, ct * 512:(ct + 1) * 512], op=ALU.add),
                        reads=["brow"], writes=["PG", "grow"])
                    P.d("sp", lambda e, vi=vi, q4=q4: e.dma_start(out=gscr[:, vi, q4 * 512:(q4 + 1) * 512], in_=grow[:, vi, :]),
                        reads=["grow"], writes=["gscr"])
                else:
                    for fs in range(4):
                        for dc in range(16):
                            P.c("pe", lambda e, dc=dc, fs=fs, w_=w_, v=v, q4=q4: e.matmul(
                                pm[:, v, q4 * 4 + fs, :], lhsT=w_[:, dc, fs * 128:(fs + 1) * 128], rhs=scT[:, dc, :],
                                start=(dc == 0), stop=(dc == 15)), reads=[wk, "scT"], writes=["PM"])
            pmv = pm[:].rearrange("p a b c -> p (a b) c")
            mdv = modT[:].rearrange("p a b c -> p (a b) c")
            for v in (0, 1, 3, 4):
                for sq in range(3):
                    P.c("dve", lambda e, sq=sq, v=v: e.tensor_tensor(out=modT[:, v, :, sq], in0=pm[:, v, :, sq], in1=bT[:, v * 16:(v + 1) * 16], op=ALU.add),
                        reads=["bT"], writes=["PM", "modT"])
            for v in (1, 4):
                P.c("dve", lambda e, v=v: e.tensor_scalar_add(modT[:, v, :, :], modT[:, v, :, :], 1.0), reads=["modT"], writes=["modT"])

        for hg in range(2):
            P.fence()
            with ExitStack() as eg:
                KT = SB(eg, "KT", [128, 4, 4096], BF16); Vt = SB(eg, "Vt", [128, 32, 512], BF16)
                ksn = SB(eg, "ksn", [128, 2, 4, 64], BF16); vsn = SB(eg, "vsn", [64, 2, 512], BF16)
                mkp = SB(eg, "mkp", [128, 4, 256], BF16); mks = SB(eg, "mks", [64, 64], BF16)
                P.d("sp", lambda e: e.dma_start(out=mkp[:], in_=maskp), writes=["mkp"])
                P.d("sp", lambda e: e.dma_start(out=mks[:], in_=masks), writes=["mks"])
                with ExitStack() as es:
                    Wq = SB(es, "Wq", [128, 16, 512], BF16); Wk = SB(es, "Wk", [128, 16, 512], BF16)
                    Wv = SB(es, "Wv", [128, 16, 512], BF16)
                    xt = SB(es, "xt", [128, 2, D], F32); uT = SB(es, "uT", [128, 2, 16, 128], BF16)
                    k32 = SB(es, "k32", [128, 2, 512], F32); v32 = SB(es, "v32", [128, 2, 512], F32)
                    tb = [PS(es, f"tb{i}", [128, 8, 128], BF16) for i in range(2)]
                    pk = PS(es, "pk", [128, 512], F32); pv = PS(es, "pv", [128, 512], F32)
                    pq = PS(es, "pq", [128, 4, 128], F32); pt = PS(es, "pt", [128, 4, 128], F32)
                    pdum1 = PS(es, "pdum1", [128, 512], F32)
                    NDUM1 = int(os.environ.get("KDUM1", "0"))
                    wload(Wq, w_in[:, hg * 512:(hg + 1) * 512], 16, 512, "Wq")
                    wload(Wk, w_in[:, 1024 + hg * 512:1024 + (hg + 1) * 512], 16, 512, "Wk")
                    wload(Wv, w_in[:, 2048 + hg * 512:2048 + (hg + 1) * 512], 16, 512, "Wv")

                    def kv_block(u_, ukey, n, m0, s2, kdst, kkey, vdst, vkey, kout, vout):
                        for dc in range(16):
                            P.c("pe", lambda e, dc=dc: e.matmul(pk[0:n, :], lhsT=u_[:, dc, m0:m0 + n], rhs=Wk[:, dc, :],
                                                                start=(dc == 0), stop=(dc == 15)), reads=[ukey, "Wk"], writes=["PK"])
                        P.c("act", lambda e: e.copy(out=k32[0:n, s2, :], in_=pk[0:n, :]), writes=["PK", f"k32{s2}"])
                        for dc in range(16):
                            P.c("pe", lambda e, dc=dc: e.matmul(pv[0:n, :], lhsT=u_[:, dc, m0:m0 + n], rhs=Wv[:, dc, :],
                                                                start=(dc == 0), stop=(dc == 15)), reads=[ukey, "Wv"], writes=["PV"])
                        P.c("dve", lambda e: e.tensor_copy(out=v32[0:n, s2, :], in_=pv[0:n, :]), writes=["PV", f"v32{s2}"])
                        P.c("pool", lambda e: e.tensor_copy(out=vdst, in_=v32[0:n, s2, :]), reads=[f"v32{s2}"], writes=[vkey])
                        P.d("sp", lambda e: e.dma_start(out=kout, in_=k32[0:n, s2, :]), reads=[f"k32{s2}"])
                        P.d("sp", lambda e: e.dma_start(out=vout, in_=v32[0:n, s2, :]), reads=[f"v32{s2}"])
                        for h in range(4):
                            P.c("pe", lambda e, h=h: e.transpose(out=pt[:, h, 0:n], in_=k32[0:n, s2, h * 128:(h + 1) * 128],
                                                                 identity=idf[0:n, 0:n]), reads=[f"k32{s2}", "idf"], writes=["PT"])
                        P.c("dve", lambda e: e.tensor_copy(out=kdst, in_=pt[:, :, 0:n]), writes=["PT", kkey])

                    items = [("g", gb) for gb in range(32)] + [("o", i) for i in range(NOWN)]
                    NI = len(items)
                    xsl = {}

                    def stageA(it):
                        kind, ix = items[it]
                        s2 = it % 2
                        src = xseq if kind == "g" else xown
                        P.d("sp", lambda e: e.dma_start(out=xt[:, s2, :], in_=src[ix * 128:(ix + 1) * 128, :]), writes=[f"xt{s2}"])
                        xsl[it] = lnA(xt[:, s2, :], f"xt{s2}", 128)

                    def stageB(it):
                        kind, ix = items[it]
                        s2 = it % 2
                        segs = [(0, 128, 0)] if kind == "g" else seg_for(ix)
                        lnB(tb, xsl[it], 128, 1, 0, segs, uT[:, s2], f"uT{s2}", 0)

                    def stageC(it):
                        kind, ix = items[it]
                        s2 = it % 2
                        if kind == "g":
                            gb = ix
                            kv_block(uT[:, s2], f"uT{s2}", 128, 0, s2, KT[:, :, gb * 128:(gb + 1) * 128], "KT",
                                     Vt[:, gb, :], "Vt",
                                     kseq[gb * 128:(gb + 1) * 128, hg * 512:(hg + 1) * 512],
                                     vseq[gb * 128:(gb + 1) * 128, hg * 512:(hg + 1) * 512])
                            return
                        i = ix
                        for h in range(4):
                            for dc in range(16):
                                P.c("pe", lambda e, h=h, dc=dc: e.matmul(
                                    pq[:, h, :], lhsT=Wq[:, dc, h * 128:(h + 1) * 128], rhs=uT[:, s2, dc, :],
                                    start=(dc == 0), stop=(dc == 15)), reads=[f"uT{s2}", "Wq"], writes=["PQ"])
                        qk = [f"Q{hg * 4 + h}_{(i // 2) if i < 16 else 8 + ss}" for h in range(4) for ss in ((0,) if i < 16 else (0, 1))]
                        P.c("act", lambda e: e.copy(out=QoT[:, hg * 4:hg * 4 + 4, i * 128:(i + 1) * 128], in_=pq[:]),
                            writes=["PQ"] + qk)
                        if i == 16:
                            for ss in range(2):
                                kv_block(uT[:, s2], f"uT{s2}", 64, ss * 64, ss, ksn[:, ss, :, :], "ksn", vsn[:, ss, :], "vsn",
                                         ksmp[ss * 64:(ss + 1) * 64, hg * 512:(hg + 1) * 512],
                                         vsmp[ss * 64:(ss + 1) * 64, hg * 512:(hg + 1) * 512])

                    for st_ in range(NI + 2):
                        for _ in range(NDUM1):
                            P.c("pe", lambda e: e.matmul(pdum1[:, :], lhsT=ones[:, :], rhs=Wq[:, 0, :], start=True, stop=True),
                                reads=["ones", "Wq"], writes=["PDUM1"])
                        if 1 <= st_ <= NI:
                            stageB(st_ - 1)
                        if st_ < NI:
                            stageA(st_)
                        if 2 <= st_:
                            stageC(st_ - 2)

                P.fence()
                with ExitStack() as es:
                    Et = SB(es, "Et", [128, 3, 256], F32); SPb = SB(es, "SPb", [128, 3, 256], BF16)
                    Xt = SB(es, "Xt", [128, 3, 256], BF16); Gt = SB(es, "Gt", [128, 2, 256], F32)
                    wT = SB(es, "wT", [128, 3, 256], BF16)
                    cst = SB(es, "cst", [128, 8, 128], F32)
                    KsT = SB(es, "KsT", [128, 4, 1024], BF16); Vs = SB(es, "Vs", [128, 8, 512], BF16)
                    Zb = [PS(es, f"Zb{i}", [128, 512], F32) for i in range(2)]
                    Ab = [PS(es, f"Ab{i}", [128, 512], F32) for i in range(2)]
                    Ob = [PS(es, f"Ob{i}", [128, 512], F32) for i in range(2)]
                    pc = PS(es, "pc", [128, 4, 128], F32)
                    pdum = PS(es, "pdum", [128, 512], F32)
                    NDUM = int(os.environ.get("KDUM", "2"))

                    def run_jobs(steps):
                        n_ = len(steps)
                        for it in range(-1, n_ + 3):
                            if it % 22 == 0:
                                conv_one()
                            j = it + 1
                            if 0 <= j < n_:
                                s = steps[j]
                                P.c("pe", lambda e, s=s, j=j: e.matmul(Zb[j % 2][0:s["nk"], 0:s["nq"]], lhsT=s["kT"], rhs=s["q"],
                                                                      start=True, stop=True),
                                    reads=[s["kkey"], s["qkey"]], writes=[f"Z{j % 2}"])
                            for _ in range(NDUM):
                                P.c("pe", lambda e: e.matmul(pdum[:, 0:256], lhsT=ones[:, :], rhs=tri[:, :].to_broadcast([128, 256]) if False else mkp[:, 0, :],
                                                             start=True, stop=True), reads=["ones", "mkp"], writes=["PDUM"])
                            j = it - 1
                            if 0 <= j < n_:
                                s = steps[j]
                                nk, nq = s["nk"], s["nq"]
                                P.c("pe", lambda e, s=s, j=j, nk=nk, nq=nq: e.matmul(
                                    Ab[j % 2][0:nk, 0:nq], lhsT=tri[0:nk, 0:nk], rhs=SPb[0:nk, j % 3, 0:nq],
                                    start=True, stop=(s["t"] == 0)), reads=[f"SP{j % 3}", "tri"], writes=[f"A{j % 2}"])
                                if s["t"] > 0:
                                    xs_ = s["xsrc"]
                                    P.c("pe", lambda e, j=j, nk=nk, nq=nq, xs_=xs_: e.matmul(
                                        Ab[j % 2][0:nk, 0:nq], lhsT=ones[:, 0:nk], rhs=Xt[:, xs_, 0:nq],
                                        start=False, stop=True), reads=[f"X{xs_}", "ones"], writes=[f"A{j % 2}"])
                            j = it - 2
                            if 0 <= j < n_:
                                s = steps[j]
                                nk, nq, ob = s["nk"], s["nq"], s["ob"]
                                P.c("pe", lambda e, s=s, j=j, nk=nk, nq=nq, ob=ob: e.matmul(
                                    Ob[ob][:, 0:nq], lhsT=s["v"], rhs=wT[0:nk, j % 3, 0:nq],
                                    start=(s["t"] == 0), stop=s["last"]), reads=[f"w{j % 3}", s["vkey"]], writes=[f"O{ob}"])
                                if s["last"]:
                                    P.c("dve", lambda e, s=s, nq=nq, ob=ob: e.tensor_copy(out=s["out"], in_=Ob[ob][:, 0:nq]),
                                        writes=[f"O{ob}", s["qkey"]])
                            j = it
                            if 0 <= j < n_:
                                s = steps[j]
                                nk, nq = s["nk"], s["nq"]
                                P.c("act", lambda e, j=j, nk=nk, nq=nq: e.activation(out=Et[0:nk, j % 3, 0:nq], in_=Zb[j % 2][0:nk, 0:nq],
                                                                                   func=AF.Exp, scale=SCALE),
                                    writes=[f"Z{j % 2}", f"E{j % 3}"])
                                P.c("act", lambda e, j=j, nk=nk, nq=nq: e.activation(out=SPb[0:nk, j % 3, 0:nq], in_=Et[0:nk, j % 3, 0:nq],
                                                                                   func=AF.Ln, bias=1.0),
                                    reads=[f"E{j % 3}"], writes=[f"SP{j % 3}"])
                                if s["mask"] is not None:
                                    P.c("pool", lambda e, s=s, j=j, nk=nk, nq=nq: e.tensor_tensor(
                                        out=SPb[0:nk, j % 3, 0:nq], in0=SPb[0:nk, j % 3, 0:nq], in1=s["mask"], op=ALU.mult),
                                        reads=["mkp", "mks"], writes=[f"SP{j % 3}"])
                                if not s["last"]:
                                    xd = s["xdst"]
                                    if s["t"] == 0:
                                        if nk < 128:
                                            P.c("pool", lambda e, xd=xd: e.memset(Xt[:, xd, :], 0.0), writes=[f"X{xd}"])
                                        P.c("pool", lambda e, j=j, nk=nk, nq=nq, xd=xd: e.tensor_copy(out=Xt[0:nk, xd, 0:nq], in_=SPb[0:nk, j % 3, 0:nq]),
                                            reads=[f"SP{j % 3}"] + ([f"X{xd}"] if nk < 128 else []), writes=[f"X{xd}"])
                                    else:
                                        xs_ = s["xsrc"]
                                        P.c("pool", lambda e, j=j, nq=nq, xd=xd, xs_=xs_: e.tensor_tensor(
                                            out=Xt[:, xd, 0:nq], in0=Xt[:, xs_, 0:nq], in1=SPb[:, j % 3, 0:nq], op=ALU.add),
                                            reads=[f"SP{j % 3}", f"X{xs_}"], writes=[f"X{xd}"])
                            j = it - 1
                            if 0 <= j < n_:
                                s = steps[j]
                                nk, nq = s["nk"], s["nq"]
                                P.c("act", lambda e, j=j, nk=nk, nq=nq: e.activation(out=Gt[0:nk, j % 2, 0:nq], in_=Ab[j % 2][0:nk, 0:nq],
                                                                                   func=AF.Exp, scale=-1.0),
                                    writes=[f"A{j % 2}", f"G{j % 2}"])
                                P.c("dve", lambda e, j=j, nk=nk, nq=nq: e.tensor_tensor(out=wT[0:nk, j % 3, 0:nq], in0=Et[0:nk, j % 3, 0:nq],
                                                                                      in1=Gt[0:nk, j % 2, 0:nq], op=ALU.mult),
                                    reads=[f"E{j % 3}", f"G{j % 2}"], writes=[f"w{j % 3}"])
                                if s["mask"] is not None:
                                    P.c("dve", lambda e, s=s, j=j, nk=nk, nq=nq: e.tensor_tensor(
                                        out=wT[0:nk, j % 3, 0:nq], in0=wT[0:nk, j % 3, 0:nq], in1=s["mask"], op=ALU.mult),
                                        reads=["mkp", "mks", f"w{j % 3}"], writes=[f"w{j % 3}"])

                    def finish(steps, njob):
                        xc = 0
                        t = 0
                        for s in steps:
                            s["t"] = t
                            s["ob"] = njob % 2
                            s["xsrc"] = (xc - 1) % 3
                            s["xdst"] = xc % 3
                            xc += 1
                            t += 1
                        return steps

                    steps = []
                    njob = 0
                    for jp in range(8):
                        for h in range(4):
                            hh = hg * 4 + h
                            job = []
                            nkb = 4 * jp + 4
                            for kb in range(nkb - 1, -1, -1):
                                r = kb - 4 * jp
                                job.append(dict(kT=KT[:, h, kb * 128:(kb + 1) * 128], v=Vt[:, kb, h * 128:(h + 1) * 128],
                                                q=QoT[:, hh, jp * 256:(jp + 1) * 256], mask=(mkp[:, r, :] if r >= 0 else None),
                                                nk=128, nq=256, last=(kb == 0), out=QoT[:, hh, jp * 256:(jp + 1) * 256],
                                                kkey="KT", vkey="Vt", qkey=f"Q{hh}_{jp}"))
                            steps += finish(job, njob)
                            njob += 1
                    run_jobs(steps)
                    for ss in range(2):
                        for h in range(4):
                            hh = hg * 4 + h
                            P.d("sp", lambda e, ss=ss, hh=hh: e.dma_start(out=cst[:], in_=ck[ss, hh].rearrange("(b p) d -> p b d", p=128)),
                                writes=["cst"])
                            for half in range(2):
                                for b4 in range(4):
                                    P.c("pe", lambda e, half=half, b4=b4: e.transpose(out=pc[:, b4, :], in_=cst[:, half * 4 + b4, :], identity=idf[:]),
                                        reads=["cst", "idf"], writes=["PC"])
                                P.c("dve", lambda e, h=h, half=half: e.tensor_copy(
                                    out=KsT[:, h, half * 512:(half + 1) * 512], in_=pc[:].rearrange("p a b -> p (a b)")),
                                    writes=["PC", "KsT"])
                            P.d("sp", lambda e, ss=ss, hh=hh: e.dma_start(out=cst[:], in_=cv[ss, hh].rearrange("(b p) d -> p b d", p=128)),
                                writes=["cst"])
                            P.c("pool", lambda e, h=h: e.tensor_copy(out=Vs[:, :, h * 128:(h + 1) * 128], in_=cst[:]), reads=["cst"], writes=["Vs"])
                        steps = []
                        for h in range(4):
                            hh = hg * 4 + h
                            q_ = QoT[:, hh, 2048 + ss * 64:2048 + (ss + 1) * 64]
                            job = [dict(kT=ksn[:, ss, h, :], v=vsn[:, ss, h * 128:(h + 1) * 128], q=q_, mask=mks[:, :], nk=64, nq=64,
                                        last=False, out=q_, kkey="ksn", vkey="vsn", qkey=f"Q{hh}_{8 + ss}")]
                            for kb in range(7, -1, -1):
                                job.append(dict(kT=KsT[:, h, kb * 128:(kb + 1) * 128], v=Vs[:, kb, h * 128:(h + 1) * 128], q=q_, mask=None,
                                                nk=128, nq=64, last=(kb == 0), out=q_, kkey="KsT", vkey="Vs", qkey=f"Q{hh}_{8 + ss}"))
                            steps += finish(job, njob)
                            njob += 1
                        run_jobs(steps)

        P.fence()
        dump("modT", modT[:], [128, 6, 16, 3], F32, ["modT"])
        dump("oT", QoT[:, :, 0:256], [128, 8, 256], BF16, [f"Q{hh}_0" for hh in range(8)])
        with ExitStack() as es:
            NWS = 4
            SMALLQ = os.environ.get("KSMALLQ", "sp")
            while conv_todo and EARLY:
                conv_one()
            Ws = [SB(es, f"Ws{i}", [128, 16, 512], BF16) for i in range(NWS)]
            wpl = SB(es, "wpl", [128, 8, 512], BF16)
            xres = SB(es, "xres", [128, ST, D], F32)
            uT = SB(es, "uT2", [128, 16, ST * 128], BF16)
            hT = SB(es, "hT", [128, 44, ST * 128], BF16)
            mT = hT[:, 0:16, :]
            dT = SB(es, "dT", [128, 8, ST * 128], BF16)
            pb = SB(es, "pb", [128, ST, 1024], BF16); pprev = SB(es, "pprev", [32, ST, 1024], BF16)
            p32 = SB(es, "p32", [128, 1024], F32)
            xstg = SB(es, "xstg", [128, D], F32); uTp = SB(es, "uTp", [128, ST, 16, 16], BF16)
            bnd = SB(es, "bnd", [128, 2, 4, 128], BF16); bnp = SB(es, "bnp", [32, 2, 4, 128], BF16)
            sA = SB(es, "sA", [128, 2, ST * 128], F32); sB = SB(es, "sB", [128, 2, ST * 128], F32)
            t1 = SB(es, "t1", [128, 2, ST * 128], F32); t2 = SB(es, "t2", [128, 2, ST * 128], F32)
            tmp = SB(es, "tmp", [128, 1, 512], F32)
            bcp = SB(es, "bcp", [128, 3, 512], F32)
            tb = [PS(es, f"tc{i}", [128, 8, 128], BF16) for i in range(2)]
            B = [PS(es, f"B{i}", [128, 512], F32) for i in range(6)]
            wload(wpl, w_pool, 8, 512, "wpl")
            if os.environ.get("KVERB"):
                print("phase2 sbuf remaining", nc.sbuf_bytes_remaining)

            def wnext(src, nk, ncols, tid):
                i = cnt["w"] % NWS
                cnt["w"] += 1
                sc_ = wscr[tid].rearrange("p (k n) -> p k n", n=512)[:, 0:nk, :]
                if tid not in wcache:
                    wcache[tid] = True
                    wload(Ws[i], src, nk, ncols, f"Ws{i}")
                    P.d("sp", lambda e: e.dma_start(out=sc_, in_=Ws[i][:, 0:nk, :]), reads=[f"Ws{i}"], writes=[f"wscr{tid}"])
                else:
                    P.d("sp", lambda e: e.dma_start(out=Ws[i][:, 0:nk, :], in_=sc_), reads=[f"wscr{tid}"], writes=[f"Ws{i}"])
                return Ws[i], f"Ws{i}"

            def bcast(src_rows, nparts):
                i = cnt["bc"] % 3
                cnt["bc"] += 1
                for (ap_, p0, p1) in src_rows:
                    P.d(SMALLQ, lambda e, ap_=ap_, p0=p0, p1=p1, i=i: e.dma_start(out=bcp[p0:p1, i, :], in_=ap_.partition_broadcast(p1 - p0)),
                        reads=["gscr"], writes=[f"bc{i}"])
                return bcp[:, i, :], f"bc{i}"

            def grows(i_blk, vi, c0):
                if i_blk < 16:
                    return [(gscr[0:1, vi, c0:c0 + 512], 0, 128)]
                return [(gscr[1:2, vi, c0:c0 + 512], 0, 64), (gscr[2:3, vi, c0:c0 + 512], 64, 128)]

            def ln_affine(i_blk, blk, gvec, bvec):
                xa = xres[:, blk, :]
                s, keys = ln_stats(xa, f"xr{blk}", 128)
                P.c("act", lambda e: e.activation(out=xa, in_=xa, func=AF.Identity, bias=nmr[:, s, :], scale=rstd[:, s, :]),
                    reads=keys, writes=[f"xr{blk}"])
                for c4 in range(4):
                    gt, gk = bcast([(gvec[0:1, c4 * 512:(c4 + 1) * 512], 0, 128)], 128)
                    bt, bk = bcast([(bvec[0:1, c4 * 512:(c4 + 1) * 512], 0, 128)], 128)
                    xs_ = xres[:, blk, c4 * 512:(c4 + 1) * 512]
                    P.c("dve", lambda e, xs_=xs_, gt=gt: e.tensor_tensor(out=xs_, in0=xs_, in1=gt, op=ALU.mult), reads=[gk], writes=[f"xr{blk}"])
                    P.c("pool", lambda e, xs_=xs_, bt=bt: e.tensor_tensor(out=xs_, in0=xs_, in1=bt, op=ALU.add), reads=[bk], writes=[f"xr{blk}"])

            tiles = [list(range(a, min(a + ST, 16))) for a in range(0, 16, ST)] + [[16]]
            if "p2" in SKIP:
                tiles = []
            def items_for(blks_):
                it_ = []
                for bi, i in enumerate(blks_):
                    if i < 16:
                        it_.append(("prev", bi, i))
                    it_.append(("main", bi, i))
                return it_

            def prepA(item):
                kind, bi, i = item
                if kind == "prev":
                    P.d("sp", lambda e: e.dma_start(out=xstg[0:16, :], in_=xprev[i * 16:(i + 1) * 16, :]), writes=["xstg"])
                    return lnA(xstg[0:16, :], "xstg", 16)
                P.d("sp", lambda e: e.dma_start(out=xstg[:, :], in_=xown[i * 128:(i + 1) * 128, :]), writes=["xstg"])
                return lnA(xstg[:, :], "xstg", 128)

            def prepB(item, xs):
                kind, bi, i = item
                if kind == "prev":
                    lnB(tb, xs, 16, 1, 0, [(0, 16, 0)], uTp[:, bi], f"uTp{bi}", 0)
                else:
                    lnB(tb, xs, 128, 1, 0, seg_for(i), uT, "uT2", bi * 128)

            if tiles:
                for item in items_for(tiles[0]):
                    prepB(item, prepA(item))
            wp_pre = None
            for ti, blks in enumerate(tiles):
                nxt_items = items_for(tiles[ti + 1]) if ti + 1 < len(tiles) else []
                nb = len(blks)
                T = nb * 128
                tok0 = blks[0] * 128
                wp = wp_pre if wp_pre is not None else [wnext(w_in[:, 3072 + c2 * 512:3072 + (c2 + 1) * 512], 16, 512, c2) for c2 in range(2)]
                for bi, i in enumerate(blks):
                    if i < 16:
                        P.c("pool", lambda e, bi=bi: e.memset(pprev[:, bi, :], 0.0), writes=[f"pprev{bi}"])
                    else:
                        P.c("pool", lambda e: e.memset(p32[0:32, :], 0.0), writes=["p32"])
                        for ss in range(2):
                            P.d(SMALLQ, lambda e, ss=ss: e.dma_start(out=p32[ss * 16:ss * 16 + 15, :], in_=spool[ss]), writes=["p32"])
                        P.c("pool", lambda e, bi=bi: e.tensor_copy(out=pprev[:, bi, :], in_=p32[0:32, :]), reads=["p32"], writes=[f"pprev{bi}"])
                    for c2 in range(2):
                        w_, wk = wp[c2]
                        for dc in range(16):
                            P.c("pe", lambda e, dc=dc, w_=w_, bi=bi: e.matmul(B[0][:, :], lhsT=uT[:, dc, bi * 128:(bi + 1) * 128], rhs=w_[:, dc, :],
                                                                              start=(dc == 0), stop=(dc == 15)), reads=["uT2", wk], writes=["B0"])
                        P.c("act", lambda e, c2=c2: e.copy(out=p32[:, c2 * 512:(c2 + 1) * 512], in_=B[0][:, :]), writes=["B0", "p32"])
                        P.c("dve", lambda e, c2=c2, bi=bi: e.tensor_copy(out=pb[:, bi, c2 * 512:(c2 + 1) * 512], in_=p32[:, c2 * 512:(c2 + 1) * 512]),
                            reads=["p32"], writes=[f"pb{bi}"])
                        if i < 16:
                            for dc in range(16):
                                P.c("pe", lambda e, dc=dc, w_=w_, bi=bi: e.matmul(B[1][0:16, :], lhsT=uTp[:, bi, dc, :], rhs=w_[:, dc, :],
                                                                                  start=(dc == 0), stop=(dc == 15)), reads=[f"uTp{bi}", wk], writes=["B1"])
                            P.c("dve", lambda e, c2=c2, bi=bi: e.tensor_copy(out=pprev[0:16, bi, c2 * 512:(c2 + 1) * 512], in_=B[1][0:16, :]),
                                writes=["B1", f"pprev{bi}"])
                    if i >= 15:
                        P.d(SMALLQ, lambda e, i=i: e.dma_start(out=pout[i - 15], in_=p32[:]), reads=["p32"])
                    bs = i % 2
                    P.d(SMALLQ, lambda e, i=i, bs=bs: e.dma_start(out=bnd[:, bs], in_=bands[i, 0:128]), writes=[f"bnd{bs}"])
                    P.d(SMALLQ, lambda e, i=i, bs=bs: e.dma_start(out=bnp[:, bs], in_=bands[i, 128:160]), writes=[f"bnd{bs}"])
                    for half in range(2):
                        for c4 in range(4):
                            cc = half * 4 + c4
                            g = cc // 2
                            P.c("pe", lambda e, cc=cc, g=g, bi=bi, bs=bs, half=half, c4=c4: e.matmul(
                                B[2 + half][:, c4 * 128:(c4 + 1) * 128], lhsT=pb[:, bi, cc * 128:(cc + 1) * 128], rhs=bnd[:, bs, g, :],
                                start=True, stop=False), reads=[f"pb{bi}", f"bnd{bs}"], writes=[f"B{2 + half}"])
                            P.c("pe", lambda e, cc=cc, g=g, bi=bi, bs=bs, half=half, c4=c4: e.matmul(
                                B[2 + half][:, c4 * 128:(c4 + 1) * 128], lhsT=pprev[:, bi, cc * 128:(cc + 1) * 128], rhs=bnp[:, bs, g, :],
                                start=False, stop=True), reads=[f"pprev{bi}", f"bnd{bs}"], writes=[f"B{2 + half}"])
                        P.c("dve", lambda e, half=half, bi=bi: e.tensor_copy(
                            out=dT[:, half * 4:half * 4 + 4, bi * 128:(bi + 1) * 128],
                            in_=B[2 + half][:, :].rearrange("p (a b) -> p a b", a=4)), writes=[f"B{2 + half}", "dT"])
                qkeys = sorted(set(f"Q{hh}_{(i // 2) if i < 16 else 8 + ss}" for hh in range(8) for i in blks
                                   for ss in ((0,) if i < 16 else (0, 1))))
                for c4 in range(4):
                    wA, kA = wnext(w_in[:, 4096 + c4 * 512:4096 + (c4 + 1) * 512], 16, 512, 2 + c4)
                    wB, kB = wnext(w_in[:, 6144 + c4 * 512:6144 + (c4 + 1) * 512], 16, 512, 6 + c4)
                    wS, kS = wnext(w_sb[:, c4 * 512:(c4 + 1) * 512], 8, 512, 10 + c4)
                    for fs in range(4):
                        fc = c4 * 4 + fs
                        s2 = fc % 2
                        fsl = slice(fs * 128, (fs + 1) * 128)
                        for dc in range(16):
                            P.c("pe", lambda e, dc=dc, wA=wA, fsl=fsl: e.matmul(B[0][:, 0:T], lhsT=wA[:, dc, fsl], rhs=uT[:, dc, 0:T],
                                                                               start=(dc == 0), stop=(dc == 15)), reads=["uT2", kA], writes=["B0"])
                        for dc in range(16):
                            P.c("pe", lambda e, dc=dc, wB=wB, fsl=fsl: e.matmul(B[1][:, 0:T], lhsT=wB[:, dc, fsl], rhs=uT[:, dc, 0:T],
                                                                               start=(dc == 0), stop=(dc == 15)), reads=["uT2", kB], writes=["B1"])
                        for h in range(8):
                            P.c("pe", lambda e, h=h, wS=wS, fsl=fsl: e.matmul(B[2][:, 0:T], lhsT=wS[:, h, fsl], rhs=QoT[:, h, tok0:tok0 + T],
                                                                             start=(h == 0), stop=(h == 7)), reads=qkeys + [kS], writes=["B2"])
                        for j2 in range(2):
                            P.c("pe", lambda e, j2=j2, c4=c4, fsl=fsl: e.matmul(B[3][:, 0:T], lhsT=wpl[:, 2 * c4 + j2, fsl], rhs=dT[:, 2 * c4 + j2, 0:T],
                                                                               start=(j2 == 0), stop=(j2 == 1)), reads=["dT", "wpl"], writes=["B3"])
                        P.c("act", lambda e, s2=s2: e.activation(out=sA[:, s2, 0:T], in_=B[0][:, 0:T], func=AF.Sigmoid), writes=["B0", f"sA{s2}"])
                        P.c("act", lambda e, s2=s2: e.activation(out=sB[:, s2, 0:T], in_=B[1][:, 0:T], func=AF.Sigmoid), writes=["B1", f"sB{s2}"])
                        P.c("dve", lambda e, s2=s2: e.tensor_tensor(out=t1[:, s2, 0:T], in0=B[2][:, 0:T], in1=sA[:, s2, 0:T], op=ALU.mult),
                            reads=[f"sA{s2}"], writes=["B2", f"t1{s2}"])
                        P.c("dve", lambda e, s2=s2, fc=fc: e.scalar_tensor_tensor(out=t2[:, s2, 0:T], in0=B[3][:, 0:T], scalar=pst[:, fc:fc + 1],
                                                                                  in1=sB[:, s2, 0:T], op0=ALU.mult, op1=ALU.mult),
                            reads=[f"sB{s2}", "pst"], writes=["B3", f"t2{s2}"])
                        P.c("pool", lambda e, s2=s2, fc=fc: e.tensor_tensor(out=mT[:, fc, 0:T], in0=t1[:, s2, 0:T], in1=t2[:, s2, 0:T], op=ALU.add),
                            reads=[f"t1{s2}", f"t2{s2}"], writes=["mT", "hT"])
                if blks[0] == 0:
                    dump("pb", pb[:], [128, ST, 1024], BF16, ["pb0", "pb1"])
                    dump("pprev", pprev[:], [32, ST, 1024], BF16, ["pprev0", "pprev1"])
                    dump("dT", dT[:, :, 0:256], [128, 8, 256], BF16, ["dT"])
                    dump("mT", mT[:, :, 0:256], [128, 16, 256], BF16, ["mT"])
                for bi, i in enumerate(blks):
                    P.d(SMALLQ, lambda e, i=i, bi=bi: e.dma_start(out=xres[:, bi, :], in_=xown[i * 128:(i + 1) * 128, :]), writes=[f"xr{bi}"])
                for c4 in range(4):
                    wO, kO = wnext(w_out[:, c4 * 512:(c4 + 1) * 512], 16, 512, 14 + c4)
                    for bi, i in enumerate(blks):
                        bb = 4 + (bi % 2)
                        for dc in range(16):
                            P.c("pe", lambda e, dc=dc, wO=wO, bi=bi, bb=bb: e.matmul(B[bb][:, :], lhsT=mT[:, dc, bi * 128:(bi + 1) * 128], rhs=wO[:, dc, :],
                                                                                    start=(dc == 0), stop=(dc == 15)), reads=["mT", kO], writes=[f"B{bb}"])
                        gt, gk = bcast(grows(i, 0, c4 * 512), 128)
                        ts = 0
                        P.c("dve", lambda e, bb=bb, gt=gt, ts=ts: e.tensor_tensor(out=tmp[:, ts, :], in0=B[bb][:, :], in1=gt, op=ALU.mult),
                            reads=[gk], writes=[f"B{bb}", f"tmp{ts}"])
                        xs_ = xres[:, bi, c4 * 512:(c4 + 1) * 512]
                        P.c("dve", lambda e, xs_=xs_, ts=ts: e.scalar_tensor_tensor(out=xs_, in0=xs_, scalar=ALPHA, in1=tmp[:, ts, :],
                                                                                    op0=ALU.mult, op1=ALU.add), reads=[f"tmp{ts}"], writes=[f"xr{bi}"])
                if blks[0] == 0:
                    dump("r1", xres[:, 0, :], [128, D], F32, ["xr0"])
                ffn_pre = (wnext(w_gate[:, 0:512], 16, 512, 18), wnext(w_up[:, 0:512], 16, 512, 29))
                for bi, i in enumerate(blks):
                    ln_affine(i, bi, ln1g, ln1b)
                xs26 = [lnA(xres[:, bi, :], f"xr{bi}", 128) for bi, i in enumerate(blks)]
                for bi, i in enumerate(blks):
                    lnB(tb, xs26[bi], 128, 4, 3, seg_for(i), uT, "uT2", bi * 128)
                if blks[0] == 0:
                    dump("x1", xres[:, 0, :], [128, D], F32, ["xr0"])
                    dump("u2T", uT[:, :, 0:256], [128, 16, 256], BF16, ["uT2"])
                for c11 in range(11):
                    if c11 == 0:
                        (wG, kG), (wU, kU) = ffn_pre
                    else:
                        wG, kG = wnext(w_gate[:, c11 * 512:(c11 + 1) * 512], 16, 512, 18 + c11)
                        wU, kU = wnext(w_up[:, c11 * 512:(c11 + 1) * 512], 16, 512, 29 + c11)
                    for fs in range(4):
                        ffc = c11 * 4 + fs
                        s2 = ffc % 2
                        fsl = slice(fs * 128, (fs + 1) * 128)
                        for dc in range(16):
                            P.c("pe", lambda e, dc=dc, wG=wG, fsl=fsl, s2=s2: e.matmul(B[s2][:, 0:T], lhsT=wG[:, dc, fsl], rhs=uT[:, dc, 0:T],
                                                                                      start=(dc == 0), stop=(dc == 15)), reads=["uT2", kG], writes=[f"B{s2}"])
                        for dc in range(16):
                            P.c("pe", lambda e, dc=dc, wU=wU, fsl=fsl, s2=s2: e.matmul(B[2 + s2][:, 0:T], lhsT=wU[:, dc, fsl], rhs=uT[:, dc, 0:T],
                                                                                      start=(dc == 0), stop=(dc == 15)), reads=["uT2", kU], writes=[f"B{2 + s2}"])
                        P.c("act", lambda e, s2=s2: e.activation(out=sA[:, s2, 0:T], in_=B[s2][:, 0:T], func=AF.Silu), writes=[f"B{s2}", f"sA{s2}"])
                        P.c("dve", lambda e, s2=s2, ffc=ffc: e.tensor_tensor(out=hT[:, ffc, 0:T], in0=B[2 + s2][:, 0:T], in1=sA[:, s2, 0:T], op=ALU.mult),
                            reads=[f"sA{s2}"], writes=[f"B{2 + s2}", "hT", "mT"])
                if blks[0] == 0:
                    dump("hT", hT[:, :, 0:256], [128, 44, 256], BF16, ["hT"])
                nxs = {}
                if nxt_items:
                    nxs[0] = prepA(nxt_items[0])
                for c4 in range(4):
                    wd = [wnext(w_down[pc_ * 2048:min((pc_ + 1) * 2048, DFF), c4 * 512:(c4 + 1) * 512], 16 if pc_ < 2 else 12, 512, 40 + c4 * 3 + pc_)
                          for pc_ in range(3)]
                    for bi, i in enumerate(blks):
                        bb = 4 + (bi % 2)
                        for ffc in range(44):
                            w_, wk = wd[ffc // 16]
                            P.c("pe", lambda e, ffc=ffc, w_=w_, bi=bi, bb=bb: e.matmul(B[bb][:, :], lhsT=hT[:, ffc, bi * 128:(bi + 1) * 128],
                                                                                      rhs=w_[:, ffc % 16, :], start=(ffc == 0), stop=(ffc == 43)),
                                reads=["hT", wk], writes=[f"B{bb}"])
                        gt, gk = bcast(grows(i, 1, c4 * 512), 128)
                        ts = 0
                        P.c("dve", lambda e, bb=bb, gt=gt, ts=ts: e.tensor_tensor(out=tmp[:, ts, :], in0=B[bb][:, :], in1=gt, op=ALU.mult),
                            reads=[gk], writes=[f"B{bb}", f"tmp{ts}"])
                        xs_ = xres[:, bi, c4 * 512:(c4 + 1) * 512]
                        P.c("dve", lambda e, xs_=xs_, ts=ts: e.scalar_tensor_tensor(out=xs_, in0=xs_, scalar=ALPHA, in1=tmp[:, ts, :],
                                                                                    op0=ALU.mult, op1=ALU.add), reads=[f"tmp{ts}"], writes=[f"xr{bi}"])
                    if c4 < len(nxt_items):
                        prepB(nxt_items[c4], nxs[c4])
                        if c4 + 1 < len(nxt_items):
                            nxs[c4 + 1] = prepA(nxt_items[c4 + 1])
                if blks[0] == 0:
                    dump("r2", xres[:, 0, :], [128, D], F32, ["xr0"])
                wp_pre = [wnext(w_in[:, 3072 + c2 * 512:3072 + (c2 + 1) * 512], 16, 512, c2) for c2 in range(2)] if nxt_items else None
                for bi, i in enumerate(blks):
                    ln_affine(i, bi, ln2g, ln2b)
                    P.d(SMALLQ, lambda e, i=i, bi=bi: e.dma_start(out=yown[i * 128:(i + 1) * 128, :], in_=xres[:, bi, :]), reads=[f"xr{bi}"])
        P.emit()
    return nc


def _own_blocks(par):
    if par == 0:
        return [4 * j + e for j in range(8) for e in (0, 3)]
    return [4 * j + e for j in range(8) for e in (1, 2)]


def _consts(par):
    bf = ml_dtypes.bfloat16
    own = _own_blocks(par)
    qoff = [0, 3] if par == 0 else [1, 2]
    s_ = np.arange(128)[:, None]
    t_ = np.arange(128)[None, :]
    maskp = np.zeros((128, 4, 256), np.float32)
    for r in range(4):
        for e in range(2):
            maskp[:, r, e * 128:(e + 1) * 128] = ((r * 128 + s_) < (qoff[e] * 128 + t_)).astype(np.float32)
    masks = (np.arange(64)[:, None] < np.arange(64)[None, :]).astype(np.float32)
    tri = (s_ >= t_).astype(np.float32)
    bands = np.zeros((NOWN, 160, 4, 128), np.float32)
    wins = (2, 4, 8, 16)
    for i in range(NOWN):
        for g, win in enumerate(wins):
            if i < 16:
                gb = own[i]
                pos = gb * 128 + np.arange(128)
                cntv = np.minimum(win, pos + 1).astype(np.float32)
                src = gb * 128 + np.arange(128)
                m = (src[:, None] > pos[None, :] - win) & (src[:, None] <= pos[None, :])
                bands[i, 0:128, g, :] = m / cntv[None, :] - np.eye(128)
                if gb > 0:
                    srcp = gb * 128 - 16 + np.arange(16)
                    mp = (srcp[:, None] > pos[None, :] - win)
                    bands[i, 128:144, g, :] = mp / cntv[None, :]
            else:
                tt = np.arange(128) % 64
                sq = np.arange(128) // 64
                m = (sq[:, None] == sq[None, :]) & (tt[:, None] > tt[None, :] - win) & (tt[:, None] <= tt[None, :])
                bands[i, 0:128, g, :] = m / float(win) - np.eye(128)
                for ss in range(2):
                    rel = np.arange(15) - 15
                    mp = (rel[:, None] > tt[None, :] - win) & (sq[None, :] == ss)
                    bands[i, 128 + ss * 16:128 + ss * 16 + 15, g, :] = mp / float(win)
    return dict(maskp=maskp.astype(bf), masks=masks.astype(bf), tri_b=tri.astype(bf),
                ones_b=np.ones((128, 128), bf), ident_b=np.eye(128).astype(bf),
                ident_f=np.eye(128, dtype=np.float32), bands=bands.astype(bf))


_NC_CACHE = {}


def kernel(x_prompt, x_sample, c_prompt, c_sample, cache_k, cache_v, state_pool, w_ada, b_ada, w_in,
           w_sb_out, w_pool, pool_scale, w_out, ln1_g, ln1_b, w_gate, w_up, w_down, ln2_g, ln2_b):
    f = lambda a: np.ascontiguousarray(np.asarray(a, dtype=np.float32))
    x_prompt, x_sample, c_prompt, c_sample = f(x_prompt), f(x_sample), f(c_prompt), f(c_sample)
    cache_k, cache_v, state_pool = f(cache_k), f(cache_v), f(state_pool)
    shared = dict(
        w_ada=f(w_ada[0]), badaT=f(np.asarray(b_ada[0]).reshape(96, 128).T), bada=f(np.asarray(b_ada[0]).reshape(1, -1)),
        w_in=f(w_in[0]), w_sb=f(w_sb_out[0]), w_pool=f(np.asarray(w_pool[0]).reshape(1024, 512)),
        psT=f(np.asarray(pool_scale[0]).reshape(16, 128).T), w_out=f(w_out[0]),
        ln1g=f(np.asarray(ln1_g[0]).reshape(1, -1)), ln1b=f(np.asarray(ln1_b[0]).reshape(1, -1)),
        ln2g=f(np.asarray(ln2_g[0]).reshape(1, -1)), ln2b=f(np.asarray(ln2_b[0]).reshape(1, -1)),
        w_gate=f(w_gate[0]), w_up=f(w_up[0]), w_down=f(w_down[0]))
    consts = [_consts(0), _consts(1)]
    in_maps = []
    for c in range(8):
        b, par = c // 2, c % 2
        own = _own_blocks(par)
        xs = x_prompt[b]
        xown = np.concatenate([xs[g * 128:(g + 1) * 128] for g in own] + [x_sample[2 * c], x_sample[2 * c + 1]], axis=0)
        xprev = np.zeros((NOWN * 16, D), np.float32)
        for i, g in enumerate(own):
            if g > 0:
                xprev[i * 16:(i + 1) * 16] = xs[g * 128 - 16:g * 128]
        c3 = np.stack([c_prompt[b], c_sample[2 * c], c_sample[2 * c + 1]], axis=0)
        cT = np.ascontiguousarray(c3.reshape(3, 16, 128).transpose(2, 1, 0))
        m = dict(xseq=xs, xown=np.ascontiguousarray(xown), xprev=xprev, cT=cT,
                 ck=np.ascontiguousarray(cache_k[0, 2 * c:2 * c + 2]), cv=np.ascontiguousarray(cache_v[0, 2 * c:2 * c + 2]),
                 spool=np.ascontiguousarray(state_pool[0, 2 * c:2 * c + 2]))
        m.update(shared)
        m.update(consts[par])
        in_maps.append(m)
    if "nc" not in _NC_CACHE:
        _NC_CACHE["nc"] = build_nc()
    res = run_bass_kernel_spmd(_NC_CACHE["nc"], in_maps, core_ids=list(range(8)))
    R = res.results
    if DEBUG:
        DBG_OUT.clear()
        DBG_OUT.update({k: v for k, v in R[0].items() if k.startswith("dbg_")})
    y_prompt = np.zeros((4, 4096, D), np.float32); y_sample = np.zeros((16, 64, D), np.float32)
    k_prompt = np.zeros((1, 4, 8, 4096, 128), np.float32); v_prompt = np.zeros_like(k_prompt)
    k_sample = np.zeros((1, 16, 8, 64, 128), np.float32); v_sample = np.zeros_like(k_sample)
    pool_prompt = np.zeros((1, 4, 15, 1024), np.float32); pool_sample = np.zeros((1, 16, 15, 1024), np.float32)
    for c in range(8):
        b, par = c // 2, c % 2
        own = _own_blocks(par)
        yo = R[c]["yown"]
        for i, g in enumerate(own):
            y_prompt[b, g * 128:(g + 1) * 128] = yo[i * 128:(i + 1) * 128]
        for ss in range(2):
            y_sample[2 * c + ss] = yo[2048 + ss * 64:2048 + (ss + 1) * 64]
            k_sample[0, 2 * c + ss] = R[c]["ksmp"][ss * 64:(ss + 1) * 64].reshape(64, 8, 128).transpose(1, 0, 2)
            v_sample[0, 2 * c + ss] = R[c]["vsmp"][ss * 64:(ss + 1) * 64].reshape(64, 8, 128).transpose(1, 0, 2)
            pool_sample[0, 2 * c + ss] = R[c]["pout"][1][ss * 64 + 49:ss * 64 + 64]
        if par == 0:
            k_prompt[0, b] = R[c]["kseq"].reshape(4096, 8, 128).transpose(1, 0, 2)
            v_prompt[0, b] = R[c]["vseq"].reshape(4096, 8, 128).transpose(1, 0, 2)
            pool_prompt[0, b] = R[c]["pout"][0][113:128]
    return (y_prompt, y_sample, k_prompt, v_prompt, pool_prompt, k_sample, v_sample, pool_sample)
```
